# Optimizing a Trainium2 kernel written in Bass

```python
import jax, jax.numpy as jnp
from jax import lax
import numpy as np

D_MODEL = 1024
BATCH = 8
SEQ = 2048
DEPTH = 4
DEC_BATCH = 128
DEC_SEQ = 4
PAST_LEN = 16384
PAGE_SIZE = 128

A_HEADS = 8
A_HEAD_DIM = D_MODEL // A_HEADS
A_WIDTH = A_HEADS * A_HEAD_DIM
B_WIDTH = D_MODEL
CONV_W = 3
CHUNK = 64
DEEPNORM_ALPHA = (2 * DEPTH) ** 0.25
DEEPNORM_BETA = (8 * DEPTH) ** -0.25
LN_EPS = 1e-5
RMS_EPS = 1e-6
LB_FLOOR = 1e-30
PROJ_SIZES = (A_WIDTH, A_WIDTH, A_WIDTH, A_WIDTH, B_WIDTH, B_WIDTH, B_WIDTH, B_WIDTH, D_MODEL, D_MODEL)
N_PROJ = sum(PROJ_SIZES)

kernel_name = "hgrn2_shortconv_gated_parallel_deepnorm_step"


def layer_norm(x, g, b):
    xf = x.astype(jnp.float32)
    mu = jnp.mean(xf, axis=-1, keepdims=True)
    var = jnp.mean(jnp.square(xf - mu), axis=-1, keepdims=True)
    return ((xf - mu) * lax.rsqrt(var + LN_EPS) * g.astype(jnp.float32) + b.astype(jnp.float32)).astype(x.dtype)


def hgrn2_lower_bounds(lb_param):
    p = jax.nn.softmax(lb_param.astype(jnp.float32), axis=0)
    return jnp.cumsum(p, axis=0) - p[0]


def hgrn2_recurrence(q, log_f, k, v, s0):
    b, t, h, dk = q.shape
    c = CHUNK if t % CHUNK == 0 else t
    n = t // c

    def to_chunks(a):
        return a.reshape(b, n, c, h, a.shape[-1]).transpose(1, 0, 3, 2, 4)

    causal = jnp.tril(jnp.ones((c, c), dtype=bool))[:, :, None]

    def step(s, inp):
        qi, gi, ki, vi = inp
        g_cum = jnp.cumsum(gi, axis=2)
        o_inter = jnp.einsum('bhtk,bhkv->bhtv', qi * jnp.exp(g_cum), s)
        diff = g_cum[:, :, :, None, :] - g_cum[:, :, None, :, :]
        decay = jnp.where(causal, jnp.exp(jnp.where(causal, diff, 0.0)), 0.0)
        attn = jnp.einsum('bhtk,bhsk,bhtsk->bhts', qi, ki, decay)
        o = o_inter + jnp.einsum('bhts,bhsv->bhtv', attn, vi)
        g_last = g_cum[:, :, -1:, :]
        k_dec = ki * jnp.exp(g_last - g_cum)
        s_new = jnp.exp(g_last[:, :, 0, :])[..., None] * s + jnp.einsum('bhsk,bhsv->bhkv', k_dec, vi)
        return s_new, o

    s_fin, o = lax.scan(step, s0, (to_chunks(q), to_chunks(log_f), to_chunks(k), to_chunks(v)))
    o = o.transpose(1, 0, 3, 2, 4).reshape(b, t, h, v.shape[-1])
    return o, s_fin


def short_conv(u, buf, w):
    t = u.shape[1]
    ext = jnp.concatenate([buf.astype(u.dtype), u], axis=1)
    y = ext[:, 0:t] * w[0]
    for j in range(1, CONV_W):
        y = y + ext[:, j:j + t] * w[j]
    return y, ext[:, -(CONV_W - 1):]


def trunk_layer(x, c, conv_buf, s0, lb, w_ada, b_ada, w_in, gnorm, conv_w, w_a_out, w_b_out, w_o, ln_g, ln_b):
    dt = x.dtype
    bsz, t, _ = x.shape
    ada = jnp.einsum('bd,de->be', jax.nn.silu(c), w_ada) + b_ada
    shift, scale, gate = jnp.split(ada, 3, axis=-1)
    h = x * (1 + scale[:, None, :]) + shift[:, None, :]
    proj = jnp.einsum('btd,de->bte', h, w_in)
    split_at = [int(s) for s in np.cumsum(PROJ_SIZES)[:-1]]
    q, fz, iv, ga, bg, cg, hv, gb, ra, rb = jnp.split(proj, split_at, axis=-1)

    def heads(a):
        return a.reshape(bsz, t, A_HEADS, A_HEAD_DIM).astype(jnp.float32)
    fz32 = heads(fz)
    lb_h = lb.reshape(A_HEADS, A_HEAD_DIM)
    log_f = jnp.logaddexp(jnp.log(jnp.maximum(lb_h, LB_FLOOR)), jnp.log1p(-lb_h) + jax.nn.log_sigmoid(fz32))
    k = (1 - lb_h) * jax.nn.sigmoid(-fz32)
    qh = jax.nn.silu(heads(q)) * (A_HEAD_DIM ** -0.5)
    o, s_new = hgrn2_recurrence(qh, log_f, k, heads(iv), s0.astype(jnp.float32))
    o = o * lax.rsqrt(jnp.mean(o * o, axis=-1, keepdims=True) + RMS_EPS) * gnorm.astype(jnp.float32).reshape(A_HEADS, A_HEAD_DIM)
    ya = jnp.einsum('btw,wd->btd', o.reshape(bsz, t, A_WIDTH).astype(dt) * jax.nn.silu(ga), w_a_out)

    v, buf_new = short_conv(cg * hv, conv_buf, conv_w)
    yb = jnp.einsum('btw,wd->btd', bg * v * jax.nn.silu(gb), w_b_out)

    m = jax.nn.sigmoid(ra) * ya + jax.nn.sigmoid(rb) * yb
    out = jnp.einsum('btd,de->bte', m, w_o)
    x_new = layer_norm(DEEPNORM_ALPHA * x + gate[:, None, :] * out, ln_g, ln_b)
    return x_new, buf_new, s_new


def run_trunk(x, c, conv_bufs, s0s, lbs, w_ada, b_ada, w_in, hgrn_gnorm, conv_w, w_a_out, w_b_out, w_o, ln_g, ln_b, state_dtype):
    hgrn_states = []
    conv_states = []
    for l in range(DEPTH):
        x, buf_new, s_new = trunk_layer(x, c, conv_bufs[l], s0s[l], lbs[l], w_ada[l], b_ada[l], w_in[l], hgrn_gnorm[l],
                                        conv_w[l], w_a_out[l], w_b_out[l], w_o[l], ln_g[l], ln_b[l])
        hgrn_states.append(s_new.astype(state_dtype))
        conv_states.append(buf_new.astype(state_dtype))
    return x, jnp.stack(hgrn_states), jnp.stack(conv_states)


def setup_inputs(seed: int = 0) -> dict:
    key = jax.random.key(seed)
    ks = jax.random.split(key, 20)
    f32 = jnp.float32
    x_prompt = jax.random.normal(ks[0], (BATCH, SEQ, D_MODEL), f32)
    x_sample = jax.random.normal(ks[1], (DEC_BATCH, DEC_SEQ, D_MODEL), f32)
    state_hgrn = 0.5 * jax.random.normal(ks[2], (DEPTH, DEC_BATCH, A_HEADS, A_HEAD_DIM, A_HEAD_DIM), f32)
    state_conv = jax.random.normal(ks[3], (DEPTH, DEC_BATCH, CONV_W - 1, B_WIDTH), f32)
    c_prompt = jax.random.normal(ks[4], (BATCH, D_MODEL), f32)
    c_sample = jax.random.normal(ks[5], (DEC_BATCH, D_MODEL), f32)
    w_ada = jax.random.normal(ks[6], (DEPTH, D_MODEL, 3 * D_MODEL), f32) * D_MODEL ** -0.5
    b_ada = 0.02 * jax.random.normal(ks[7], (DEPTH, 3 * D_MODEL), f32)
    w_in = jax.random.normal(ks[8], (DEPTH, D_MODEL, N_PROJ), f32) * D_MODEL ** -0.5
    hgrn_lb = 0.1 * jax.random.normal(ks[9], (DEPTH, A_WIDTH), f32)
    hgrn_gnorm = 1.0 + 0.01 * jax.random.normal(ks[10], (DEPTH, A_WIDTH), f32)
    conv_w = jax.random.normal(ks[11], (DEPTH, CONV_W, B_WIDTH), f32) * CONV_W ** -0.5
    w_a_out = jax.random.normal(ks[12], (DEPTH, A_WIDTH, D_MODEL), f32) * (A_WIDTH ** -0.5) * DEEPNORM_BETA
    w_b_out = jax.random.normal(ks[13], (DEPTH, B_WIDTH, D_MODEL), f32) * (B_WIDTH ** -0.5) * DEEPNORM_BETA
    w_o = jax.random.normal(ks[14], (DEPTH, D_MODEL, D_MODEL), f32) * (D_MODEL ** -0.5) * DEEPNORM_BETA
    ln_g = 1.0 + 0.01 * jax.random.normal(ks[15], (DEPTH, D_MODEL), f32)
    ln_b = 0.01 * jax.random.normal(ks[16], (DEPTH, D_MODEL), f32)
    return {"x_prompt": x_prompt, "x_sample": x_sample, "state_hgrn": state_hgrn, "state_conv": state_conv,
            "c_prompt": c_prompt, "c_sample": c_sample, "w_ada": w_ada, "b_ada": b_ada, "w_in": w_in,
            "hgrn_lb": hgrn_lb, "hgrn_gnorm": hgrn_gnorm, "conv_w": conv_w, "w_a_out": w_a_out,
            "w_b_out": w_b_out, "w_o": w_o, "ln_g": ln_g, "ln_b": ln_b}


def reference(x_prompt, x_sample, state_hgrn, state_conv, c_prompt, c_sample, w_ada, b_ada, w_in, hgrn_lb,
              hgrn_gnorm, conv_w, w_a_out, w_b_out, w_o, ln_g, ln_b):
    lbs = hgrn2_lower_bounds(hgrn_lb)
    sd = state_hgrn.dtype
    zero_hgrn = jnp.zeros((DEPTH, BATCH, A_HEADS, A_HEAD_DIM, A_HEAD_DIM), jnp.float32)
    zero_conv = jnp.zeros((DEPTH, BATCH, CONV_W - 1, B_WIDTH), x_prompt.dtype)
    y_prompt, new_hgrn_prompt, new_conv_prompt = run_trunk(
        x_prompt, c_prompt, zero_conv, zero_hgrn, lbs, w_ada, b_ada, w_in, hgrn_gnorm, conv_w,
        w_a_out, w_b_out, w_o, ln_g, ln_b, sd)
    y_sample, new_hgrn_sample, new_conv_sample = run_trunk(
        x_sample, c_sample, state_conv, state_hgrn, lbs, w_ada, b_ada, w_in, hgrn_gnorm, conv_w,
        w_a_out, w_b_out, w_o, ln_g, ln_b, sd)
    return (y_prompt, y_sample, new_hgrn_prompt, new_conv_prompt, new_hgrn_sample, new_conv_sample)
```

```python
import numpy as np
from contextlib import ExitStack
import concourse.bass as bass
import concourse.mybir as mybir
from concourse.bass_utils import run_bass_kernel_spmd

F32 = mybir.dt.float32
BF16 = mybir.dt.bfloat16
F32R = mybir.dt.float32r
I32 = mybir.dt.int32
AF = mybir.ActivationFunctionType
ALU = mybir.AluOpType

DEPTH = 4
NCORES = 8
SEQ = 2048
NSS = 16
TS = 4
NTOK = SEQ + NSS * TS
HALF_P = 1024
HALF_S = 32
HTOK = HALF_P + HALF_S
ALPHA = (2 * DEPTH) ** 0.25
LN_EPS = 1e-5
RMS_EPS = 1e-6
QSCALE = 128 ** -0.5
SCHED_DEBUG = False
SCHED_XLAT = 0.25
SCHED_WINDOW = 0.0
PE_SWITCH = 0.0


class _Op:
    __slots__ = ("id", "eng", "fn", "reads", "writes", "dma", "semkey", "deps", "signals", "sigidx", "grp")


class Prog:
    def __init__(self):
        self.ops = []
        self.last_writer = {}
        self.readers = {}

    def add(self, eng, fn, reads=(), writes=(), dma=False, semkey=None, grp=None):
        op = _Op()
        op.grp = grp
        if grp is not None:
            grp.append(op)
        op.id = len(self.ops); op.eng = eng; op.fn = fn
        op.reads = tuple(reads); op.writes = tuple(writes)
        op.dma = dma; op.semkey = semkey; op.signals = False; op.sigidx = None
        deps = set()
        for r in op.reads:
            w = self.last_writer.get(r)
            if w is not None:
                deps.add(w)
        for r in op.writes:
            w = self.last_writer.get(r)
            if w is not None:
                deps.add(w)
            for rd in self.readers.get(r, ()):
                deps.add(rd)
        deps.discard(op.id)
        op.deps = deps
        for r in op.reads:
            self.readers.setdefault(r, []).append(op.id)
        for r in op.writes:
            self.last_writer[r] = op.id
            self.readers[r] = []
        self.ops.append(op)
        return op

    def plan(self):
        ops = self.ops
        pos = {}; cnt = {}
        for op in ops:
            cnt[op.eng] = cnt.get(op.eng, 0) + 1
            pos[op.id] = cnt[op.eng]
        need = {}
        for op in ops:
            lst = []
            for d in op.deps:
                dop = ops[d]
                if dop.dma or dop.eng != op.eng:
                    lst.append(d)
                else:
                    if op.eng == "pe":
                        continue
                    lst.append(d)
            need[op.id] = lst
            for d in lst:
                ops[d].signals = True
        sigcnt = {}
        for op in ops:
            if op.dma:
                k = op.semkey
                sigcnt[k] = sigcnt.get(k, 0) + 16
                op.sigidx = sigcnt[k]
            elif op.signals:
                sigcnt[op.eng] = sigcnt.get(op.eng, 0) + 1
                op.sigidx = sigcnt[op.eng]
        self.sigcnt = sigcnt
        waited = {}
        streams = {}
        for op in ops:
            w = {}
            for d in need[op.id]:
                dop = ops[d]
                k = dop.semkey if dop.dma else dop.eng
                v = dop.sigidx if dop.grp is None else max(g.sigidx for g in dop.grp)
                if v > w.get(k, 0):
                    w[k] = v
            wl = []
            for k, v in w.items():
                if waited.get((op.eng, k), 0) < v:
                    waited[(op.eng, k)] = v
                    wl.append((k, v))
            streams.setdefault(op.eng, []).append((op, wl))
        self.streams = streams
        self.semkeys = sorted(set(sigcnt.keys()) | {"pe", "act", "dve", "pool"})

    def emit_engine(self, eng, e, sems):
        for op, wl in self.streams.get(eng, ()):
            for k, v in wl:
                e.wait_ge(sems[k], v)
            if op.fn is None:
                continue
            inst = op.fn(e)
            if op.dma:
                inst.then_inc(sems[op.semkey], 16)
            elif op.signals:
                inst.then_inc(sems[op.eng], 1)


def build_nc(depth=DEPTH):
    nc = bass.Bass("TRN2", target_bir_lowering=False)
    dt_in = lambda name, shape: nc.dram_tensor(name, list(shape), F32, kind="ExternalInput").ap()
    dt_out = lambda name, shape: nc.dram_tensor(name, list(shape), F32, kind="ExternalOutput").ap()
    xp = dt_in("xp", (SEQ, 1024)); xs = dt_in("xs", (NSS * TS, 1024))
    sh = dt_in("sh", (DEPTH, NSS, 8, 128, 128)); scv = dt_in("scv", (DEPTH * NSS * 2, 1024))
    cc = dt_in("cc", (1 + NSS, 1024))
    w_ada = dt_in("w_ada", (DEPTH, 1024, 3072)); b_ada = dt_in("b_ada", (DEPTH * 3, 1024))
    w_in = dt_in("w_in", (DEPTH, 1024, 10240))
    lbp = dt_in("hgrn_lb", (DEPTH, 1024)); gnp = dt_in("hgrn_gnorm", (DEPTH, 1024))
    cwp = dt_in("conv_w", (DEPTH * 3, 1024))
    w_a = dt_in("w_a_out", (DEPTH, 1024, 1024)); w_b = dt_in("w_b_out", (DEPTH, 1024, 1024))
    w_o = dt_in("w_o", (DEPTH, 1024, 1024))
    lngp = dt_in("ln_g", (DEPTH, 1024)); lnbp = dt_in("ln_b", (DEPTH, 1024))
    yp = dt_out("yp", (SEQ, 1024)); ys = dt_out("ys", (NSS * TS, 1024))
    nhp = dt_out("nhp", (DEPTH, 8, 128, 128)); ncp = dt_out("ncp", (DEPTH, 16, 128))
    nhs = dt_out("nhs", (DEPTH, NSS, 8, 128, 128)); ncs = dt_out("ncs", (DEPTH, 256, 128))

    es = ExitStack()
    sb = lambda name, shape, dt=F32: es.enter_context(nc.sbuf_tensor(name, list(shape), dt))
    X = sb("X", (128, 8, NTOK))
    H = sb("H", (128, 8, HTOK), BF16)
    AB = sb("AB", (128, 8, HTOK), BF16)
    M = sb("M", (128, 8, HTOK), BF16)
    WS = [sb(f"WS{i}", (128, 4096), BF16) for i in range(3)]
    Tall = sb("Tall", (128, 10 * 512))
    T = [[Tall[:, (p * 5 + i) * 512:(p * 5 + i + 1) * 512] for i in range(5)] for p in range(2)]
    B16all = sb("B16all", (128, 16 * 512), BF16)
    B16 = [[B16all[:, (p * 8 + i) * 512:(p * 8 + i + 1) * 512] for i in range(8)] for p in range(2)]
    U0 = sb("U0", (128, 516))
    U = [U0, U0]
    ATT = sb("ATT", (128, 4, 128), BF16)
    SBF = sb("SBF", (128, 8, 128), BF16)
    SR = [sb(f"SR{i}", (128, 128)) for i in range(3)]
    ATTS = sb("ATTS", (32, 32), BF16)
    S = sb("S", (128, 8, 128))
    KS2 = sb("KS2", (128, 256), BF16)
    S0 = [sb(f"S0_{i}", (128, 128)) for i in range(4)]
    SB0 = [sb(f"SB0_{i}", (128, 128), BF16) for i in range(4)]
    SN = [sb(f"SN_{i}", (128, 128)) for i in range(4)]
    MF = M[:].rearrange("p k n -> p (k n)").bitcast(F32)
    XIN = [MF[:, 2 * i * 528:(2 * i + 2) * 528].rearrange("p (g n) -> p g n", n=528)[:, :, 0:512] for i in range(4)]
    XKG = [[[("M", 2 * i + g, t) for t in range(3)] for g in range(2)] for i in range(4)]
    XK = [XKG[i][0] + XKG[i][1] for i in range(4)]
    IDF = sb("IDF", (128, 128)); IDB = sb("IDB", (128, 128), BF16)
    ONESF = sb("ONESF", (128, 128)); ONESB = sb("ONESB", (128, 128), BF16)
    ONE512 = sb("ONE512", (128, 512))
    CM = sb("CM", (128, 64)); CMS = sb("CMS", (32, 32)); SEL = sb("SEL", (32, 8))
    WF1 = WS[1][:].bitcast(F32); WF2 = WS[2][:].bitcast(F32)
    PRT = WF2[0:40, 1024:2048]
    PAR = sb("PAR", (128, 8, 40))
    LBT = sb("LBT", (128, 8, 16))
    LB = sb("LB", (128, 8, 4)); OML = sb("OML", (128, 8, 4)); NOML = sb("NOML", (128, 8, 4))
    CST = sb("CST", (128, 4))
    CCT = WF1[0:17, 0:1024]; CSB = sb("CSB", (128, 8, 17), BF16)
    SCT = WF2[:, 0:1024]; CH = sb("CH", (128, 8, 128))
    W1K = [("W", 1, i) for i in range(4)]; W2K = [("W", 2, i) for i in range(4)]
    ADA = sb("ADA", (128, 4, 8, 17))
    GCAR = sb("GCAR", (128, 8))
    PEV = [sb(f"PEV{i}", (128, 9)) for i in range(2)]
    SV = [sb(f"SV{i}", (128, 48)) for i in range(2)]
    SVE = [sb(f"SVE{i}", (128, 48)) for i in range(2)]
    UH = sb("UH", (128, 8, 2))
    US = sb("US", (128, 8, 6))
    COP = sb("COP", (128, 2, 8)); COS = sb("COS", (128, 16, 2, 8))
    COPT = T[1][4][0:16, 0:128]; COST = T[0][4][:, 0:256].rearrange("p (g n) -> p g n", n=128)
    PS = [es.enter_context(nc.psum_tensor(f"PS{i}", [128, 512], F32)) for i in range(7)]
    PSB = es.enter_context(nc.psum_tensor("PSB", [128, 1024], BF16))
    def pk(b):
        return [("ps", b, r) for r in range(4)]

    P = Prog()
    rec_target = [None]

    META = ("cost", "aset", "lat", "pmode")

    def A(eng, fn, reads=(), writes=(), **kw):
        if rec_target[0] is not None:
            rec_target[0].append((eng, fn, tuple(reads), tuple(writes), kw))
        else:
            P.add(eng, fn, reads, writes, **{k: v for k, v in kw.items() if k not in META})

    def fsz(ap):
        n = 1
        for d in ap.shape[1:]:
            n *= int(d)
        return n

    ASETS = {AF.Sigmoid: ("S",), AF.Silu: ("I",), AF.Ln: ("L",), AF.Exp: ("L", "E")}

    def emit_scheduled(blk):
        n = len(blk)
        lw, rds = {}, {}
        preds = [set() for _ in range(n)]
        for i, (eng, fn, R, W, kw) in enumerate(blk):
            for r in R:
                w = lw.get(r)
                if w is not None:
                    preds[i].add(w)
            for r in W:
                w = lw.get(r)
                if w is not None:
                    preds[i].add(w)
                preds[i].update(rds.get(r, ()))
            preds[i].discard(i)
            for r in R:
                rds.setdefault(r, []).append(i)
            for r in W:
                lw[r] = i
                rds[r] = []
        succs = [[] for _ in range(n)]
        indeg = [0] * n
        for i in range(n):
            indeg[i] = len(preds[i])
            for p in preds[i]:
                succs[p].append(i)
        cost = [blk[i][4].get("cost", 0.2) for i in range(n)]
        lat = [blk[i][4].get("lat", 0.0) for i in range(n)]
        prio = [0.0] * n
        for i in range(n - 1, -1, -1):
            m = 0.0
            for sx in succs[i]:
                if prio[sx] > m:
                    m = prio[sx]
            prio[i] = cost[i] + lat[i] + m
        etime = {}
        fin = [0.0] * n
        cur_set = [None]
        cur_pm = [None]
        ready = [i for i in range(n) if indeg[i] == 0]
        order = []
        while ready:
            cands = []
            tmin = None
            for i in ready:
                eng = blk[i][0]
                td = 0.0
                for p in preds[i]:
                    t = fin[p] + (0.05 if blk[p][0] == eng and not blk[p][4].get("dma") else SCHED_XLAT)
                    if t > td:
                        td = t
                t = max(etime.get(eng, 0.0), td)
                sw = 0.0
                aset = blk[i][4].get("aset")
                if aset is not None and cur_set[0] not in aset:
                    sw = 1.28
                pm = blk[i][4].get("pmode")
                if pm is not None and pm != cur_pm[0]:
                    sw = PE_SWITCH
                cands.append((t + sw, i, sw))
                if tmin is None or t + sw < tmin:
                    tmin = t + sw
            best = None
            for (t, i, sw) in cands:
                if t <= tmin + SCHED_WINDOW:
                    if best is None or prio[i] > prio[best[1]]:
                        best = (t, i, sw)
            t, i, sw = best
            eng = blk[i][0]
            aset = blk[i][4].get("aset")
            if aset is not None and cur_set[0] not in aset:
                cur_set[0] = aset[0]
            if blk[i][4].get("pmode") is not None:
                cur_pm[0] = blk[i][4]["pmode"]
            if blk[i][4].get("dma"):
                etime[eng] = t + cost[i]
                fin[i] = t + cost[i] + lat[i]
            else:
                etime[eng] = t + cost[i]
                fin[i] = t + cost[i]
            order.append(i)
            ready.remove(i)
            for sx in succs[i]:
                indeg[sx] -= 1
                if indeg[sx] == 0:
                    ready.append(sx)
        assert len(order) == n
        for i in order:
            eng, fn, R, W, kw = blk[i]
            P.add(eng, fn, R, W, **{k: v for k, v in kw.items() if k not in META})
        return max(fin) if n else 0.0

    def record(f, *args):
        lst = []
        rec_target[0] = lst
        f(*args)
        rec_target[0] = None
        return lst

    def replay(*lists):
        idx = [0] * len(lists)
        while True:
            best, bf = None, None
            for i, l in enumerate(lists):
                if idx[i] < len(l):
                    fr = idx[i] / len(l)
                    if bf is None or fr < bf:
                        best, bf = i, fr
            if best is None:
                break
            eng, fn, R, W, kw = lists[best][idx[best]]
            idx[best] += 1
            P.add(eng, fn, R, W, **kw)

    pending_tail = []

    def act(out, in_, func, R, W, bias=None, scale=None):
        kw = {}
        if bias is not None: kw["bias"] = bias
        if scale is not None: kw["scale"] = scale
        A("act", lambda e: e.activation(out=out, in_=in_, func=func, **kw), R, W,
          cost=0.2 + fsz(out) * 0.00078, aset=ASETS.get(func))

    def ecost(eng, ap, mult=1.0):
        n = fsz(ap)
        if eng == "pool":
            return 0.25 + n * 0.0016
        return 0.12 + n * 0.00105 * mult

    def tt(eng, out, in0, in1, op, R, W):
        A(eng, lambda e: e.tensor_tensor(out=out, in0=in0, in1=in1, op=op), R, W, cost=ecost(eng, out))

    def tsc(eng, out, in0, s1, s2, op0, op1, R, W):
        A(eng, lambda e: e.tensor_scalar(out=out, in0=in0, scalar1=s1, scalar2=s2, op0=op0, op1=op1), R, W, cost=ecost(eng, out))

    def stt(out, in0, scalar, in1, op0, op1, R, W):
        A("dve", lambda e: e.scalar_tensor_tensor(out=out, in0=in0, scalar=scalar, in1=in1, op0=op0, op1=op1), R, W,
          cost=ecost("dve", out, 1.15))

    def cp(eng, out, in_, R, W):
        if eng == "act":
            A("act", lambda e: e.copy(out=out, in_=in_), R, W, cost=0.2 + fsz(out) * 0.00078)
        else:
            A(eng, lambda e: e.tensor_copy(out=out, in_=in_), R, W, cost=ecost(eng, out))

    def r32(v):
        return 32 if v <= 32 else (64 if v <= 64 else 128)

    def mm(out, lhsT, rhs, start, stop, R, W):
        n = fsz(out)
        pmode = (r32(int(lhsT.shape[0])), r32(fsz(lhsT)))
        A("pe", lambda e: e.matmul(out=out, lhsT=lhsT, rhs=rhs, start=start, stop=stop), R, W,
          cost=(n * 0.00052 if n >= 256 else (0.115 if n >= 128 else 0.14)) * (4.0 if rhs.dtype == F32 else 1.0), pmode=pmode)

    def tr(out, in_, ident, R, W):
        A("pe", lambda e: e.transpose(out=out, in_=in_, identity=ident), R, W, cost=0.2,
          pmode=(r32(int(in_.shape[0])), r32(fsz(in_)), "T"))

    def dma(q, out, in_, R, W, semkey, grp=None, lat=2.5):
        A(q, lambda e: e.dma_start(out=out, in_=in_), R, W, dma=True, semkey=semkey, grp=grp,
          cost=(1.06 if q == "pool" else 0.06), lat=lat)

    def mset(eng, ap, val, W):
        A(eng, lambda e: e.memset(ap, val), (), W, cost=ecost(eng, ap))

    out_res = []
    blk_all = []
    rec_target[0] = blk_all

    mset("pool", IDF[:], 0.0, ["IDF"])
    A("pool", lambda e: e.affine_select(out=IDF[:], in_=IDF[:], pattern=[[-1, 128]], compare_op=ALU.not_equal,
                                         fill=1.0, base=0, channel_multiplier=1), ["IDF"], ["IDF"])
    cp("pool", IDB[:], IDF[:], ["IDF"], ["IDB"])
    mset("pool", ONESF[:], 1.0 / 1024.0, ["ONESF"])
    mset("pool", ONESB[:], 1.0, ["ONESB"])
    mset("pool", ONE512[:], 1.0, ["ONE512"])
    mset("pool", ATT[:], 0.0, [("ATT", 0), ("ATT", 1)])
    mset("pool", ATTS[:], 0.0, ["ATTS"])
    mset("pool", CST[:, 0:1], RMS_EPS, ["CST"])
    mset("pool", CST[:, 1:2], LN_EPS / (ALPHA * ALPHA), ["CST"])
    mset("pool", CM[:], 1.0, ["CM"])
    for hb in range(2):
        A("pool", lambda e, hb=hb: e.affine_select(out=CM[hb * 64:(hb + 1) * 64, :], in_=CM[hb * 64:(hb + 1) * 64, :],
                                                   pattern=[[1, 64]], compare_op=ALU.is_ge, fill=0.0, base=0,
                                                   channel_multiplier=-1), ["CM"], ["CM"])
    for hb in range(2):
        mset("pool", CM[hb * 64:hb * 64 + 32, 32:64], 0.0, ["CM"])
    mset("pool", SEL[:], 1.0, ["SEL"])
    A("pool", lambda e: e.affine_select(out=SEL[:], in_=SEL[:], pattern=[[-4, 8]], compare_op=ALU.is_ge, fill=0.0,
                                         base=0, channel_multiplier=1), ["SEL"], ["SEL"])
    A("pool", lambda e: e.affine_select(out=SEL[:], in_=SEL[:], pattern=[[4, 8]], compare_op=ALU.is_ge, fill=0.0,
                                         base=3, channel_multiplier=-1), ["SEL"], ["SEL"])
    cp("pool", CMS[:].rearrange("p (j t) -> p j t", t=4), SEL[:].unsqueeze(2).broadcast_to([32, 8, 4]), ["SEL"], ["CMS"])
    A("pool", lambda e: e.affine_select(out=CMS[:], in_=CMS[:], pattern=[[1, 32]], compare_op=ALU.is_ge, fill=0.0,
                                         base=0, channel_multiplier=-1), ["CMS"], ["CMS"])
    CMi = CM[:].bitcast(I32); CMSi = CMS[:].bitcast(I32)

    prm = [(lbp, 0, 4), (gnp, 4, 4), (cwp, 8, 12), (lngp, 20, 4), (lnbp, 24, 4), (b_ada, 28, 12)]
    for i, (src, r0, nr) in enumerate(prm):
        dma("sp", WF2[r0:r0 + nr, 1024:2048], src, [], W2K, f"ldp{i}")
    for k in range(8):
        tr(PS[0][:, k * 40:(k + 1) * 40], WF2[0:40, 1024 + k * 128:1024 + (k + 1) * 128], IDF[0:40, 0:40],
           W2K + ["IDF"], pk(0))
    cp("dve", PAR[:], PS[0][:, 0:320].rearrange("p (k r) -> p k r", r=40), pk(0), ["PAR"])
    tt("dve", LBT[:, :, 0:1], PAR[:, :, 0:1], PAR[:, :, 1:2], ALU.max, ["PAR"], ["LBT"])
    tt("dve", LBT[:, :, 1:2], PAR[:, :, 2:3], PAR[:, :, 3:4], ALU.max, ["PAR"], ["LBT1"])
    tt("dve", LBT[:, :, 0:1], LBT[:, :, 0:1], LBT[:, :, 1:2], ALU.max, ["LBT", "LBT1"], ["LBT"])
    tt("dve", LBT[:, :, 4:8], PAR[:, :, 0:4], LBT[:, :, 0:1].broadcast_to([128, 8, 4]), ALU.subtract, ["PAR", "LBT"], ["LBT2"])
    act(LBT[:, :, 8:12], LBT[:, :, 4:8], AF.Exp, ["LBT2"], ["LBT3"])
    tt("dve", LBT[:, :, 1:2], LBT[:, :, 8:9], LBT[:, :, 9:10], ALU.add, ["LBT3", "LBT1"], ["LBT1"])
    tt("dve", LBT[:, :, 2:3], LBT[:, :, 10:11], LBT[:, :, 11:12], ALU.add, ["LBT3"], ["LBT4"])
    tt("dve", LBT[:, :, 1:2], LBT[:, :, 1:2], LBT[:, :, 2:3], ALU.add, ["LBT1", "LBT4"], ["LBT1"])
    A("dve", lambda e: e.reciprocal(out=LBT[:, :, 3:4], in_=LBT[:, :, 1:2]), ["LBT1"], ["LBT5"])
    tt("dve", LBT[:, :, 12:16], LBT[:, :, 8:12], LBT[:, :, 3:4].broadcast_to([128, 8, 4]), ALU.mult, ["LBT3", "LBT5"], ["LBT6"])
    mset("dve", LB[:, :, 0:1], 0.0, ["LB0"])
    cp("dve", LB[:, :, 1:2], LBT[:, :, 13:14], ["LBT6"], ["LB1"])
    tt("dve", LB[:, :, 2:3], LB[:, :, 1:2], LBT[:, :, 14:15], ALU.add, ["LB1", "LBT6"], ["LB2"])
    tt("dve", LB[:, :, 3:4], LB[:, :, 2:3], LBT[:, :, 15:16], ALU.add, ["LB2", "LBT6"], ["LB3"])
    LBall = ["LB0", "LB1", "LB2", "LB3"]
    tsc("dve", OML[:], LB[:], -1.0, 1.0, ALU.mult, ALU.add, LBall, ["OML"])
    tsc("dve", NOML[:], LB[:], 1.0, -1.0, ALU.mult, ALU.add, LBall, ["NOML"])
    LBR = LBall + ["OML", "NOML"]

    dma("sp", CCT, cc, [], W1K, "ldc")
    for k in range(8):
        tr(PS[1][:, k * 17:(k + 1) * 17], WF1[0:17, k * 128:(k + 1) * 128], IDF[0:17, 0:17], W1K + ["IDF"], pk(1))
    act(CSB[:], PS[1][:, 0:136].rearrange("p (k r) -> p k r", r=17), AF.Silu, pk(1), ["CSB"])
    dma("sp", SCT, scv, [], W2K, "ldsc")
    for g in range(2):
        for k4 in range(4):
            k = g * 4 + k4
            tr(PS[2 + g][:, k4 * 128:(k4 + 1) * 128], WF2[:, k * 128:(k + 1) * 128], IDF[:], W2K + ["IDF"], pk(2 + g))
        cp("dve", CH[:, g * 4:(g + 1) * 4, :], PS[2 + g][:].rearrange("p (k r) -> p k r", r=128), pk(2 + g), [("CH", g)])
    CHR = [("CH", 0), ("CH", 1)]

    def xkeys(k, col0, n):
        return [("X", k, b) for b in range(col0 // 128, (col0 + n + 127) // 128)]

    nblk = SEQ // 128 + 1
    for b in range(nblk):
        sl = b % 4
        rows = 128 if b < SEQ // 128 else NSS * TS
        src = xp[b * 128:(b + 1) * 128, :] if b < SEQ // 128 else xs
        dma("sp", XIN[sl][0:rows, :, :], src.rearrange("p (g n) -> p g n", g=2), [], XK[sl], f"ldx{sl}")
        for g in range(2):
            bank = 5 + g if False else (2 + g)
            for k4 in range(4):
                k = g * 4 + k4
                tr(PS[bank][:, k4 * 128:k4 * 128 + rows], XIN[sl][0:rows, g, k4 * 128:(k4 + 1) * 128], IDF[0:rows, 0:rows],
                   XKG[sl][g] + ["IDF"], pk(bank))
            eng = "act" if g == 0 else "dve"
            cp(eng, X[:, g * 4:(g + 1) * 4, b * 128:b * 128 + rows],
               PS[bank][:].rearrange("p (k r) -> p k r", r=128)[:, :, 0:rows], pk(bank),
               [("X", k, b) for k in range(g * 4, g * 4 + 4)])

    stages = []

    def wload(slot, pieces):
        grp = []
        for i, (d, s) in enumerate(pieces):
            dma("pool", d, s, [], [("W", slot, i)], f"w{slot}", grp=grp, lat=7.0)

    def wkeys(slot, n):
        return [("W", slot, i) for i in range(n)]

    def wsrc(wt, l, c0, n):
        return wt[l, :, c0:c0 + n].rearrange("(k p) n -> p k n", p=128)

    def tiles_of(hf):
        return [("p", hf * HALF_P, 0, 512), ("p", hf * HALF_P + 512, 512, 512), ("s", SEQ + hf * HALF_S, HALF_P, HALF_S)]

    tcount = [0]

    def hkeys(name, k, lc, n):
        return [(name, k, lc // 512)]

    def ada_stage(l, g):
        def load(slot):
            wv = WS[slot][:].rearrange("p (k n) -> p k n", n=512)
            wload(slot, [(wv, wsrc(w_ada, l, g * 512, 512))])

        def comp(slot):
            wv = WS[slot][:].rearrange("p (k n) -> p k n", n=512)
            for mc in range(4):
                ec = g * 4 + mc
                part, chunk = ec // 8, ec % 8
                bank = mc % 2
                for k in range(8):
                    mm(PS[bank][:, 0:17], wv[:, k, mc * 128:(mc + 1) * 128], CSB[:, k, :], k == 0, k == 7,
                       wkeys(slot, 4) + ["CSB"], pk(bank))
                bcol = PAR[:, chunk, 28 + l * 3 + part:28 + l * 3 + part + 1]
                if part == 0:
                    tsc("dve", ADA[:, 0, chunk, :], PS[bank][:, 0:17], bcol, None, ALU.add, ALU.bypass, pk(bank) + ["PAR"], [("ADA", 0, chunk)])
                elif part == 1:
                    tsc("dve", ADA[:, 1, chunk, :], PS[bank][:, 0:17], bcol, 1.0, ALU.add, ALU.add, pk(bank) + ["PAR"], [("ADA", 1, chunk)])
                else:
                    tsc("dve", ADA[:, 2 + l % 2, chunk, :], PS[bank][:, 0:17], bcol, 1.0 / ALPHA, ALU.add, ALU.mult, pk(bank) + ["PAR"], [("ADA", 2 + l % 2, chunk)])
        return load, comp

    def h_stage(l, hf):
        def comp():
            for k in range(8):
                eng = "dve" if k % 2 == 0 else "pool"
                tsc(eng, H[:, k, 0:HALF_P], X[:, k, hf * HALF_P:(hf + 1) * HALF_P], ADA[:, 1, k, 0:1], ADA[:, 0, k, 0:1],
                    ALU.mult, ALU.add, xkeys(k, hf * HALF_P, HALF_P) + [("ADA", 1, k), ("ADA", 0, k)],
                    [("H", k, 0), ("H", k, 1)])
            xs_ = X[:, :, SEQ + hf * HALF_S:SEQ + (hf + 1) * HALF_S].rearrange("p k (s t) -> p k s t", t=TS)
            tmp = T[0][0][:, 0:256].rearrange("p (k s t) -> p k s t", s=8, t=TS)
            sc = ADA[:, 1, :, 1 + hf * 8:9 + hf * 8].unsqueeze(3).broadcast_to([128, 8, 8, TS])
            shf = ADA[:, 0, :, 1 + hf * 8:9 + hf * 8].unsqueeze(3).broadcast_to([128, 8, 8, TS])
            adk = [("ADA", pp, k) for pp in range(2) for k in range(8)]
            xk = [("X", k, 16) for k in range(8)]
            tt("dve", tmp, xs_, sc, ALU.mult, xk + adk, [("T", 0, 0)])
            tt("dve", H[:, :, HALF_P:HTOK].rearrange("p k (s t) -> p k s t", t=TS), tmp, shf, ALU.add,
               [("T", 0, 0)] + adk, [("H", k, 2) for k in range(8)])
        return comp

    acount = [0]

    def a_stage(l, hf, j):
        def load(slot):
            wv = WS[slot][:].rearrange("p (k b n) -> p k b n", b=4, n=128)
            wload(slot, [(wv[:, :, b, :], wsrc(w_in, l, b * 1024 + j * 128, 128)) for b in range(4)])

        def comp(slot):
            wv = WS[slot][:].rearrange("p (k b n) -> p k b n", b=4, n=128)
            WK = wkeys(slot, 4)
            lbv = LB[:, j, l:l + 1]; omlv = OML[:, j, l:l + 1]; nomlv = NOML[:, j, l:l + 1]
            gnv = PAR[:, j, 4 + l:5 + l]
            Sj = S[:, j, :]
            if hf == 0:
                mset("pool", Sj, 0.0, [("S", j)])
                mset("pool", GCAR[:, j:j + 1], 0.0, [("GCAR", j)])
            cx = []
            for ti, (kind, xc0, lc0, n) in enumerate(tiles_of(hf)):
                pb = acount[0] % 2
                acount[0] += 1
                cx.append((ti, kind, xc0, lc0, n, pb))

            def unpack(c):
                ti, kind, xc0, lc0, n, pb = cx[c]
                Tt = T[0]
                QS, KS, KDT, KD, VT, SG, OSQ, QI = B16[pb]
                TK = lambda i: ("T", 0, i)
                BK = lambda i: ("B", pb, i)
                nst = max(1, n // 128)
                rows = 128 if kind == "p" else n
                return ti, kind, xc0, lc0, n, pb, Tt, QS, KS, KDT, KD, VT, SG, OSQ, QI, TK, BK, nst, rows

            def inproj(c):
                ti, kind, xc0, lc0, n, pb, Tt, QS, KS, KDT, KD, VT, SG, OSQ, QI, TK, BK, nst, rows = unpack(c)
                HK = [("H", k, ti) for k in range(8)]
                for blk, bank in ((1, 1), (0, 0), (3, 2)):
                    for k in range(8):
                        mm(PS[bank][:, 0:n], wv[:, k, blk, :], H[:, k, lc0:lc0 + n], k == 0, k == 7, WK + [HK[k]], pk(bank))
                for st in range(nst):
                    for k in range(8):
                        mm(PS[3][0:rows, st * 128:(st + 1) * 128], H[:, k, lc0 + st * 128:lc0 + st * 128 + rows], wv[:, k, 2, :],
                           k == 0, k == 7, WK + [HK[k]], pk(3))

            def aphase(c):
                ti, kind, xc0, lc0, n, pb, Tt, QS, KS, KDT, KD, VT, SG, OSQ, QI, TK, BK, nst, rows = unpack(c)
                act(Tt[0][:, 0:n], PS[1][:, 0:n], AF.Sigmoid, pk(1), [TK(0)])
                act(Tt[4][:, 0:n], PS[0][:, 0:n], AF.Sigmoid, pk(0), [TK(4)])
                act(Tt[1][:, 0:n], PS[2][:, 0:n], AF.Sigmoid, pk(2), [TK(1)])
                cp("dve", VT[0:rows, 0:nst * 128], PS[3][0:rows, 0:nst * 128], pk(3), [BK(4)])
                tt("dve", Tt[4][:, 0:n], PS[0][:, 0:n], Tt[4][:, 0:n], ALU.mult, pk(0) + [TK(4)], [TK(4)])
                tt("dve", SG[:, 0:n], PS[2][:, 0:n], Tt[1][:, 0:n], ALU.mult, pk(2) + [TK(1)], [BK(5)])
                tsc("pool", Tt[3][:, 0:n], Tt[0][:, 0:n], nomlv, omlv, ALU.mult, ALU.add, [TK(0)] + LBR, [TK(3)])
                act(Tt[0][:, 0:n], Tt[0][:, 0:n], AF.Ln, [TK(0)] + LBR, [TK(0)], bias=lbv, scale=omlv)
                if kind == "p":
                    A("dve", lambda e, Tt=Tt, j=j: e.tensor_tensor_scan(out=Tt[1][:, 0:512], data0=ONE512[:], data1=Tt[0][:, 0:512],
                                                                        initial=GCAR[:, j:j + 1], op0=ALU.mult, op1=ALU.add),
                      [TK(0), "ONE512", ("GCAR", j)], [TK(1)], cost=1.3)
                    G3 = Tt[1][:].rearrange("p (c t) -> p c t", t=64)
                    G3h = Tt[1][:].rearrange("p (h t) -> p h t", t=32)
                    mid82 = Tt[1][:].rearrange("p (c h t) -> p c h t", h=2, t=32)[:, :, :, 15]
                    cp("dve", PEV[pb][:, 0:1], GCAR[:, j:j + 1], [("GCAR", j)], [("PEV", pb)])
                    cp("dve", PEV[pb][:, 1:9], G3[:, :, 63], [TK(1)], [("PEV", pb)])
                    cp("dve", GCAR[:, j:j + 1], Tt[1][:, 511:512], [TK(1)], [("GCAR", j)])
                    tt("dve", SV[pb][:, 0:16].rearrange("p (c h) -> p c h", h=2), G3[:, :, 63:64].broadcast_to([128, 8, 2]), mid82,
                       ALU.subtract, [TK(1)], [("SV", pb)])
                    tt("dve", SV[pb][:, 16:24], PEV[pb][:, 1:9], PEV[pb][:, 0:8], ALU.subtract, [("PEV", pb)], [("SV", pb)])
                    tt("dve", SV[pb][:, 24:40].rearrange("p (c h) -> p c h", h=2), mid82,
                       PEV[pb][:, 0:8].unsqueeze(2).broadcast_to([128, 8, 2]), ALU.subtract, [TK(1), ("PEV", pb)], [("SV", pb)])
                    tt("dve", SV[pb][:, 40:48], mid82[:, :, 1], mid82[:, :, 0], ALU.subtract, [TK(1)], [("SV", pb)])
                    act(SVE[pb][:], SV[pb][:], AF.Exp, [("SV", pb)], [("SVE", pb)])
                    tt("dve", Tt[2][:].rearrange("p (h t) -> p h t", t=32), G3h, G3h[:, :, 15:16].broadcast_to([128, 16, 32]),
                       ALU.subtract, [TK(1)], [TK(2)])
                    act(Tt[1][:], Tt[2][:], AF.Exp, [TK(2)], [TK(1)])
                    act(Tt[2][:], Tt[2][:], AF.Exp, [TK(2)], [TK(2)], scale=-1.0)
                    EQ, EK = Tt[1], Tt[2]
                    ccb = SVE[pb][:, 0:16].unsqueeze(2).broadcast_to([128, 16, 32])
                    v3 = lambda ap: ap[:, 0:512].rearrange("p (h t) -> p h t", t=32)
                else:
                    L3 = Tt[0][:, 0:n].rearrange("p (s t) -> p s t", t=TS)
                    G3 = Tt[1][:, 0:n].rearrange("p (s t) -> p s t", t=TS)
                    cp("dve", G3[:, :, 0:1], L3[:, :, 0:1], [TK(0)], [TK(1)])
                    for t in range(1, TS):
                        tt("dve", G3[:, :, t:t + 1], G3[:, :, t - 1:t], L3[:, :, t:t + 1], ALU.add, [TK(0), TK(1)], [TK(1)])
                    act(Tt[2][:, 0:n], Tt[1][:, 0:n], AF.Exp, [TK(1)], [TK(2)])
                    act(Tt[1][:, 0:n], Tt[1][:, 0:n], AF.Exp, [TK(1)], [TK(1)], scale=-1.0)
                    EQ, EK = Tt[2], Tt[1]
                    E3 = Tt[2][:, 0:n].rearrange("p (s t) -> p s t", t=TS)
                    cp("dve", SVE[pb][:, 0:8], E3[:, :, TS - 1], [TK(2)], [("SVE", pb)])
                    ccb = SVE[pb][:, 0:8].unsqueeze(2).broadcast_to([128, 8, TS])
                    v3 = lambda ap: ap[:, 0:n].rearrange("p (s t) -> p s t", t=TS)
                stt(QS[:, 0:n], Tt[4][:, 0:n], QSCALE, EQ[:, 0:n], ALU.mult, ALU.mult, [TK(4), TK(1), TK(2)], [BK(0)])
                tt("dve", KS[:, 0:n], Tt[3][:, 0:n], EK[:, 0:n], ALU.mult, [TK(3), TK(1), TK(2)], [BK(1)])
                tt("pool", v3(KDT), v3(KS), ccb, ALU.mult, [BK(1), ("SVE", pb)], [BK(2)])
                if kind == "p":
                    tt("pool", KS2[:].rearrange("p (c t) -> p c t", t=32), KS[:].rearrange("p (c t) -> p c t", t=64)[:, :, 0:32],
                       SVE[pb][:, 40:48].unsqueeze(2).broadcast_to([128, 8, 32]), ALU.mult, [BK(1), ("SVE", pb)], ["KS2"])
                    tt("pool", v3(QI), v3(QS), SVE[pb][:, 24:40].unsqueeze(2).broadcast_to([128, 16, 32]), ALU.mult,
                       [BK(0), ("SVE", pb)], [BK(7)])

            def recA(c):
                ti, kind, xc0, lc0, n, pb, Tt, QS, KS, KDT, KD, VT, SG, OSQ, QI, TK, BK, nst, rows = unpack(c)
                KDv = KD[:].rearrange("p (s n) -> p s n", n=128)
                VTv = VT[:].rearrange("p (s n) -> p s n", n=128)
                for st in range(nst):
                    tr(PSB[0:rows, st * 128:(st + 1) * 128], KDT[:, st * 128:st * 128 + rows], IDB[:], [BK(2), "IDB"], ["psb"])
                cp("act", KD[0:rows, 0:nst * 128], PSB[0:rows, 0:nst * 128], ["psb"], [BK(3)])
                def supd(ch):
                    hb, st = ch % 2, ch // 2
                    p0 = hb * 64
                    xr = [("RT2", pb)] if hb == 1 else []
                    xw = [("RT1", pb)] if hb == 0 else []
                    mm(PS[5][:, (ch % 4) * 128:(ch % 4) * 128 + 128], KDv[p0:p0 + 64, st, :], VTv[p0:p0 + 64, st, :], True, True,
                       [BK(3), BK(4)] + xr, pk(5) + xw)
                if kind == "p":
                    supd(0); supd(2)
                    for ch in range(8):
                        hb, cc = ch % 2, ch // 2
                        p0 = hb * 64
                        cs = slice(ch * 64, (ch + 1) * 64)
                        csB = slice(ch * 64 + 32, ch * 64 + 64)
                        mm(PS[4][p0:p0 + 64, cc * 128:cc * 128 + 64], KS[:, cs], QS[:, cs], True, True, [BK(0), BK(1)], [("ps", 4, hb)])
                        mm(PS[4][p0:p0 + 32, cc * 128 + 64:cc * 128 + 96], KS2[:, ch * 32:(ch + 1) * 32], QS[:, csB], True, True,
                           [BK(0), "KS2"] + ([("RT1", pb)] if ch == 7 else []), [("ps", 4, hb)] + ([("RT2", pb)] if ch == 7 else []))
                    for hb in range(2):
                        p0 = hb * 64
                        A("dve", lambda e, p0=p0: e.copy_predicated(
                            out=ATT[p0:p0 + 64, :, p0:p0 + 64], mask=CMi[p0:p0 + 64, :].unsqueeze(1).broadcast_to([64, 4, 64]),
                            data=PS[4][p0:p0 + 64, :].rearrange("p (c t) -> p c t", t=128)[:, :, 0:64]),
                          [("ps", 4, hb), "CM"], [("ATT", hb)], cost=0.45)
                        cp("act", ATT[p0:p0 + 32, :, p0 + 32:p0 + 64], PS[4][p0:p0 + 32, :].rearrange("p (c t) -> p c t", t=128)[:, :, 64:96],
                           [("ps", 4, hb)], [("ATT", hb)])
                    supd(1); supd(3)
                else:
                    mm(PS[4][0:n, 0:n], KS[:, 0:n], QS[:, 0:n], True, True, [BK(0), BK(1)], [("ps", 4, 0)])
                    A("dve", lambda e, n=n: e.copy_predicated(out=ATTS[:], mask=CMSi, data=PS[4][0:n, 0:n]), [("ps", 4, 0), "CMS"], ["ATTS"])
                    VM = B16all[0:n, (pb * 8 + 6) * 512:(pb * 8 + 8) * 512].rearrange("p (s v) -> p s v", v=128)
                    tt("dve", VM, VTv[0:n, 0, :].unsqueeze(1).broadcast_to([n, 8, 128]), SEL[:].unsqueeze(2).broadcast_to([n, 8, 128]),
                       ALU.mult, [BK(4), "SEL"], [BK(6), BK(7)])

            def chain(c):
                ti, kind, xc0, lc0, n, pb, Tt, QS, KS, KDT, KD, VT, SG, OSQ, QI, TK, BK, nst, rows = unpack(c)
                KDv = KD[:].rearrange("p (s n) -> p s n", n=128)
                VTv = VT[:].rearrange("p (s n) -> p s n", n=128)
                def supd2(c2):
                    hb2, st2 = c2 % 2, c2 // 2
                    q0 = hb2 * 64
                    xr = [("RT4", pb)] if hb2 == 1 else []
                    xw = [("RT3", pb)] if hb2 == 0 else []
                    mm(PS[5][:, (c2 % 4) * 128:(c2 % 4) * 128 + 128], KDv[q0:q0 + 64, st2, :], VTv[q0:q0 + 64, st2, :],
                       True, True, [BK(3), BK(4)] + xr, pk(5) + xw)
                if kind == "p":
                    for ch in range(8):
                        hb, st = ch % 2, ch // 2
                        p0 = hb * 64
                        cs = slice(ch * 64, (ch + 1) * 64)
                        sprev, kprev = (Sj, ("S", j)) if ch == 0 else (SR[ch % 3][:], ("SR", ch % 3))
                        snext, knext = (Sj, ("S", j)) if ch == 7 else (SR[(ch + 1) % 3][:], ("SR", (ch + 1) % 3))
                        cp("act", SBF[:, ch, :], sprev, [kprev], [("SBF", ch)])
                        if hb == 0:
                            mm(PS[6][:, st * 128:(st + 1) * 128], VTv[:, st, :], ATT[:, st, :], True, False,
                               [BK(4), ("ATT", 0), ("ATT", 1)], pk(6))
                        mm(PS[6][:, cs], SBF[:, ch, :], QI[:, cs], False, hb == 1,
                           [("SBF", ch), BK(7)] + ([("RT3", pb)] if ch == 4 else []), pk(6) + ([("RT4", pb)] if ch == 4 else []))
                        stt(snext, sprev, SVE[pb][:, 16 + ch:17 + ch], PS[5][:, (ch % 4) * 128:(ch % 4) * 128 + 128], ALU.mult, ALU.add,
                            [kprev, ("SVE", pb), ("ps", 5, ch % 4)], [knext])
                        if ch == 3:
                            supd2(4); supd2(6)
                        if ch == 4:
                            supd2(5); supd2(7)
                    if hf == 1 and ti == 1:
                        dma("sp", nhp[l, j], Sj, [("S", j)], [("nhp", l, j)], f"sthp{j}")
                        out_res.append(("nhp", l, j))
                else:
                    VM = B16all[0:n, (pb * 8 + 6) * 512:(pb * 8 + 8) * 512].rearrange("p (s v) -> p s v", v=128)
                    mm(PS[6][:, 0:n], VTv[0:n, 0, :], ATTS[:], True, False, [BK(4), "ATTS"], pk(6))
                    for sb4 in range(2):
                        for sq in range(sb4 * 4, sb4 * 4 + 4):
                            gs = hf * 8 + sq
                            sl = sq % 4
                            if sb4 == 1:
                                dma("sp", S0[sl][:], sh[l, gs, j], [], [("S0", sl)], f"lds{sl}")
                            cp("pool", SB0[sl][:], S0[sl][:], [("S0", sl)], [("SB0", sl)])
                            mm(PS[6][:, sq * TS:(sq + 1) * TS], SB0[sl][:], QS[:, sq * TS:(sq + 1) * TS], False, sq == 7,
                               [("SB0", sl), BK(0)], pk(6))
                        for sq in range(sb4 * 4, sb4 * 4 + 4):
                            r = sq % 4
                            mm(PS[5][:, r * 128:r * 128 + 128], KDv[0:n, 0, :], VM[:, sq, :], True, True, [BK(3), BK(6), BK(7)], pk(5))
                        for sq in range(sb4 * 4, sb4 * 4 + 4):
                            gs = hf * 8 + sq
                            sl = sq % 4
                            r = sq % 4
                            stt(SN[sl][:], S0[sl][:], SVE[pb][:, sq:sq + 1], PS[5][:, r * 128:r * 128 + 128], ALU.mult, ALU.add,
                                [("S0", sl), ("SVE", pb), ("ps", 5, r)], [("SN", sl)])
                            dma("sp", nhs[l, gs, j], SN[sl][:], [("SN", sl)], [("nhs", sl)], f"stsn{sl}")
                            if ("nhs", sl) not in out_res:
                                out_res.append(("nhs", sl))

            def norm(c):
                ti, kind, xc0, lc0, n, pb, Tt, QS, KS, KDT, KD, VT, SG, OSQ, QI, TK, BK, nst, rows = unpack(c)
                RSn = B16all[:, (pb * 8 + 0) * 512:(pb * 8 + 2) * 512].bitcast(F32)
                T1n = B16all[:, (pb * 8 + 2) * 512:(pb * 8 + 4) * 512].bitcast(F32)
                K01 = [BK(0), BK(1)]; K23 = [BK(2), BK(3)]
                act(OSQ[:, 0:n], PS[6][:, 0:n], AF.Square, pk(6), [BK(6)])
                mm(PS[4][:, 0:n], ONESB[:], OSQ[:, 0:n], True, True, [BK(6), "ONESB"], pk(4))
                act(RSn[:, 0:n], PS[4][:, 0:n], AF.Ln, pk(4) + ["CST"], K01, bias=CST[:, 0:1], scale=1.0 / 128.0)
                act(RSn[:, 0:n], RSn[:, 0:n], AF.Exp, K01, K01, scale=-0.5)
                stt(T1n[:, 0:n], PS[6][:, 0:n], gnv, RSn[:, 0:n], ALU.mult, ALU.mult, pk(6) + ["PAR"] + K01, K23)
                tt("dve", AB[:, j, lc0:lc0 + n], T1n[:, 0:n], SG[:, 0:n], ALU.mult, K23 + [BK(5)], [("AB", j, ti)])

            for sq in range(4):
                dma("sp", S0[sq][:], sh[l, hf * 8 + sq, j], [], [("S0", sq)], f"lds{sq}")
            for c in range(3):
                inproj(c); aphase(c); recA(c); chain(c); norm(c)
        return load, comp

    mbank = [0]

    def merge_stage(l, hf, i, which):
        wsrc_y = w_a if which == 0 else w_b
        rcol = 8192 + which * 1024 + i * 128

        def load(slot):
            wv = WS[slot][:, 0:2048].rearrange("p (k b n) -> p k b n", b=2, n=128)
            wload(slot, [(wv[:, :, 0, :], wsrc(wsrc_y, l, i * 128, 128)), (wv[:, :, 1, :], wsrc(w_in, l, rcol, 128))])

        def comp(slot):
            wv = WS[slot][:, 0:2048].rearrange("p (k b n) -> p k b n", b=2, n=128)
            WK = wkeys(slot, 4)
            for ti, (kind, xc0, lc0, n) in enumerate(tiles_of(hf)):
                pb = tcount[0] % 2
                tcount[0] += 1
                Tt = T[pb]
                TK = lambda ii: ("T", pb, ii)
                ba, bb = ((0, 1), (2, 3))[mbank[0] % 2]
                mbank[0] += 1
                for k in range(8):
                    mm(PS[ba][:, 0:n], wv[:, k, 0, :], AB[:, k, lc0:lc0 + n], k == 0, k == 7, WK + [("AB", k, ti)], pk(ba))
                for k in range(8):
                    mm(PS[bb][:, 0:n], wv[:, k, 1, :], H[:, k, lc0:lc0 + n], k == 0, k == 7, WK + [("H", k, ti)], pk(bb))
                act(Tt[0][:, 0:n], PS[bb][:, 0:n], AF.Sigmoid, pk(bb), [TK(0)])
                if which == 0:
                    tt("dve", M[:, i, lc0:lc0 + n], PS[ba][:, 0:n], Tt[0][:, 0:n], ALU.mult, pk(ba) + [TK(0)], [("M", i, ti)])
                else:
                    tt("dve", Tt[1][:, 0:n], PS[ba][:, 0:n], Tt[0][:, 0:n], ALU.mult, pk(ba) + [TK(0)], [TK(1)])
                    tt("pool", M[:, i, lc0:lc0 + n], M[:, i, lc0:lc0 + n], Tt[1][:, 0:n], ALU.add, [("M", i, ti), TK(1)], [("M", i, ti)])
        return load, comp

    def b_stage(l, hf, j):
        def load(slot):
            wv = WS[slot][:].rearrange("p (k b n) -> p k b n", b=4, n=128)
            wload(slot, [(wv[:, :, b, :], wsrc(w_in, l, 4096 + b * 1024 + j * 128, 128)) for b in range(4)])

        def comp(slot):
            wv = WS[slot][:].rearrange("p (k b n) -> p k b n", b=4, n=128)
            WK = wkeys(slot, 4)
            w0 = PAR[:, j, 8 + l * 3:9 + l * 3]; w1 = PAR[:, j, 9 + l * 3:10 + l * 3]; w2 = PAR[:, j, 10 + l * 3:11 + l * 3]
            if hf == 0:
                mset("pool", UH[:, j, :], 0.0, [("UH", j)])
            for ti, (kind, xc0, lc0, n) in enumerate(tiles_of(hf)):
                pb = 0
                Tt = T[1]
                TK = lambda ii: ("T", 1, ii)
                HK = [("H", k, ti) for k in range(8)]
                for blk in (2, 1, 3, 0):
                    for k in range(8):
                        mm(PS[blk][:, 0:n], wv[:, k, blk, :], H[:, k, lc0:lc0 + n], k == 0, k == 7, WK + [HK[k]], pk(blk))
                cp("act", Tt[0][:, 0:n], PS[2][:, 0:n], pk(2), [TK(0)])
                act(Tt[2][:, 0:n], PS[3][:, 0:n], AF.Sigmoid, pk(3), [TK(2)])
                tt("dve", Tt[2][:, 0:n], PS[3][:, 0:n], Tt[2][:, 0:n], ALU.mult, pk(3) + [TK(2)], [TK(2)])
                if kind == "p":
                    Ut = U[pb]
                    cp("pool", Ut[:, 0:2], UH[:, j, :], [("UH", j)], [("U", 0)])
                    tt("dve", Ut[:, 2:2 + n], PS[1][:, 0:n], Tt[0][:, 0:n], ALU.mult, pk(1) + [TK(0)], [("U", 0)])
                    cp("pool", UH[:, j, :], Ut[:, n:n + 2], [("U", 0)], [("UH", j)])
                    if hf == 1 and ti == 1:
                        cp("pool", COP[:, :, j], Ut[:, n:n + 2], [("U", 0)], [("COP", j)])
                    tsc("dve", Tt[1][:, 0:n], Ut[:, 0:n], w0, None, ALU.mult, ALU.bypass, [("U", 0), "PAR"], [TK(1)])
                    stt(Tt[1][:, 0:n], Ut[:, 1:n + 1], w1, Tt[1][:, 0:n], ALU.mult, ALU.add, [("U", 0), "PAR", TK(1)], [TK(1)])
                    stt(Tt[1][:, 0:n], Ut[:, 2:n + 2], w2, Tt[1][:, 0:n], ALU.mult, ALU.add, [("U", 0), "PAR", TK(1)], [TK(1)])
                else:
                    r0 = l * 32 + hf * 16
                    cp("pool", US[:, :, 0:2], CH[:, j, r0:r0 + 16].rearrange("p (s t) -> p s t", t=2), CHR, ["US"])
                    tt("dve", US[:, :, 2:6], PS[1][:, 0:n].rearrange("p (s t) -> p s t", t=TS),
                       Tt[0][:, 0:n].rearrange("p (s t) -> p s t", t=TS), ALU.mult, pk(1) + [TK(0)], ["US"])
                    cp("pool", COS[:, hf * 8:(hf + 1) * 8, :, j], US[:, :, 4:6], ["US"], [("COS", j)])
                    t3 = Tt[1][:, 0:n].rearrange("p (s t) -> p s t", t=TS)
                    tsc("dve", t3, US[:, :, 0:4], w0, None, ALU.mult, ALU.bypass, ["US", "PAR"], [TK(1)])
                    stt(t3, US[:, :, 1:5], w1, t3, ALU.mult, ALU.add, ["US", "PAR", TK(1)], [TK(1)])
                    stt(t3, US[:, :, 2:6], w2, t3, ALU.mult, ALU.add, ["US", "PAR", TK(1)], [TK(1)])
                tt("dve", Tt[3][:, 0:n], PS[0][:, 0:n], Tt[1][:, 0:n], ALU.mult, pk(0) + [TK(1)], [TK(3)])
                tt("pool", AB[:, j, lc0:lc0 + n], Tt[3][:, 0:n], Tt[2][:, 0:n], ALU.mult, [TK(3), TK(2)], [("AB", j, ti)])
        return load, comp

    def conv_out(l):
        def comp():
            tr(PS[0][0:16, 0:128], COP[:].rearrange("p t j -> p (t j)"), IDF[:], [("COP", j) for j in range(8)] + ["IDF"], pk(0))
            cp("dve", COPT, PS[0][0:16, 0:128], pk(0), [("T", 1, 4)])
            dma("sp", ncp[l], COPT, [("T", 1, 4)], ["ncp"], "stcp")
            cosf = COS[:].rearrange("p s t j -> p (s t j)")
            for g in range(2):
                tr(PS[1][:, g * 128:(g + 1) * 128], cosf[:, g * 128:(g + 1) * 128], IDF[:], [("COS", j) for j in range(8)] + ["IDF"], pk(1))
            cp("dve", COST, PS[1][:, 0:256].rearrange("p (g n) -> p g n", n=128), pk(1), [("T", 0, 4)])
            dma("sp", ncs[l].rearrange("(g r) n -> r g n", g=2), COST, [("T", 0, 4)], ["ncs"], "stcs")
        return comp
    out_res.extend(["ncp", "ncs"])

    def o_stage(l, hf, i):
        def load(slot):
            wv = WS[slot][:, 0:1024].rearrange("p (k n) -> p k n", n=128)
            wload(slot, [(wv, wsrc(w_o, l, i * 128, 128))])

        def comp(slot):
            wv = WS[slot][:, 0:1024].rearrange("p (k n) -> p k n", n=128)
            WK = wkeys(slot, 4)
            for ti, (kind, xc0, lc0, n) in enumerate(tiles_of(hf)):
                pb = tcount[0] % 2
                tcount[0] += 1
                Tt = T[pb]
                TK = lambda ii: ("T", pb, ii)
                ba = (0, 1, 2, 3)[mbank[0] % 4]
                mbank[0] += 1
                for k in range(8):
                    mm(PS[ba][:, 0:n], wv[:, k, :], M[:, k, lc0:lc0 + n], k == 0, k == 7, WK + [("M", k, ti)], pk(ba))
                xk = xkeys(i, xc0, n)
                if kind == "p":
                    stt(X[:, i, xc0:xc0 + n], PS[ba][:, 0:n], ADA[:, 2 + l % 2, i, 0:1], X[:, i, xc0:xc0 + n], ALU.mult, ALU.add,
                        pk(ba) + [("ADA", 2 + l % 2, i)] + xk, xk)
                else:
                    gt = ADA[:, 2 + l % 2, i, 1 + hf * 8:9 + hf * 8].unsqueeze(2).broadcast_to([128, 8, TS])
                    t3 = Tt[0][:, 0:n].rearrange("p (s t) -> p s t", t=TS)
                    tt("dve", t3, PS[ba][:, 0:n].rearrange("p (s t) -> p s t", t=TS), gt, ALU.mult, pk(ba) + [("ADA", 2 + l % 2, i)], [TK(0)])
                    tt("dve", X[:, i, xc0:xc0 + n], X[:, i, xc0:xc0 + n], Tt[0][:, 0:n], ALU.add, [TK(0)] + xk, xk)
        return load, comp

    def ln_stage(l, hf):
        def comp():
            for ti, (kind, xc0, lc0, n) in enumerate(tiles_of(hf)):
                pb = tcount[0] % 2
                tcount[0] += 1
                Tt = T[pb]
                TK = lambda ii: ("T", pb, ii)
                for k in range(8):
                    mm(PS[0][:, 0:n], ONESF[:], X[:, k, xc0:xc0 + n], k == 0, k == 7, ["ONESF"] + xkeys(k, xc0, n), pk(0))
                for k in range(8):
                    sq = Tt[3 + k % 2].bitcast(BF16)
                    act(sq[:, 0:n], X[:, k, xc0:xc0 + n], AF.Square, xkeys(k, xc0, n), [TK(3 + k % 2)])
                    mm(PS[1][:, 0:n], ONESB[:], sq[:, 0:n], k == 0, k == 7, ["ONESB", TK(3 + k % 2)], pk(1))
                cp("act", Tt[0][:, 0:n], PS[0][:, 0:n], pk(0), [TK(0)])
                stt(Tt[1][:, 0:n], Tt[0][:, 0:n], -1.0, Tt[0][:, 0:n], ALU.mult, ALU.mult, [TK(0)], [TK(1)])
                stt(Tt[1][:, 0:n], PS[1][:, 0:n], 1.0 / 1024.0, Tt[1][:, 0:n], ALU.mult, ALU.add, [TK(1)] + pk(1), [TK(1)])
                act(Tt[1][:, 0:n], Tt[1][:, 0:n], AF.Ln, [TK(1), "CST"], [TK(1)], bias=CST[:, 1:2], scale=1.0)
                act(Tt[1][:, 0:n], Tt[1][:, 0:n], AF.Exp, [TK(1)], [TK(1)], scale=-0.5)
                for k in range(8):
                    tm = Tt[3 + k % 2]
                    xk = xkeys(k, xc0, n)
                    tt("pool", tm[:, 0:n], X[:, k, xc0:xc0 + n], Tt[0][:, 0:n], ALU.subtract, xk + [TK(0)], [TK(3 + k % 2)])
                    tt("dve", tm[:, 0:n], tm[:, 0:n], Tt[1][:, 0:n], ALU.mult, [TK(3 + k % 2), TK(1)], [TK(3 + k % 2)])
                    act(X[:, k, xc0:xc0 + n], tm[:, 0:n], AF.Identity, [TK(3 + k % 2), "PAR"], xk,
                        bias=PAR[:, k, 24 + l:25 + l], scale=PAR[:, k, 20 + l:21 + l])
        return comp

    for l in range(depth):
        for g in range(6):
            stages.append(ada_stage(l, g))
        for hf in range(2):
            stages.append((None, h_stage(l, hf)))
            for j in range(8):
                stages.append(a_stage(l, hf, j) + ("A",))
            for i in range(8):
                stages.append(merge_stage(l, hf, i, 0))
            for j in range(8):
                stages.append(b_stage(l, hf, j) + ("B",))
            if hf == 1:
                stages.append((None, conv_out(l)))
            for i in range(8):
                stages.append(merge_stage(l, hf, i, 1))
            for i in range(8):
                stages.append(o_stage(l, hf, i))
            stages.append((None, ln_stage(l, hf)))

    wst = [s for s in stages if s[0] is not None]
    widx = {id(s): n for n, s in enumerate(wst)}
    PRE = 2
    for n in range(min(PRE, len(wst))):
        wst[n][0](n % 3)
    for s in stages:
        if s[0] is None:
            s[1]()
        else:
            n = widx[id(s)]
            if n + PRE < len(wst):
                wst[n + PRE][0]((n + PRE) % 3)
            s[1](n % 3)

    for b in range(nblk):
        sl = b % 4
        rows = 128 if b < SEQ // 128 else NSS * TS
        dst = yp[b * 128:(b + 1) * 128, :] if b < SEQ // 128 else ys
        for g in range(2):
            bank = 2 + g
            for k4 in range(4):
                k = g * 4 + k4
                tr(PS[bank][0:rows, k4 * 128:(k4 + 1) * 128], X[:, k, b * 128:b * 128 + rows], IDF[:], [("X", k, b), "IDF"], pk(bank))
            eng = "act" if g == 0 else "dve"
            cp(eng, XIN[sl][0:rows, g, :], PS[bank][0:rows, :], pk(bank), XKG[sl][g])
        dma("sp", dst.rearrange("p (g n) -> p g n", g=2), XIN[sl][0:rows, :, :], XK[sl], [("yout", sl)], f"sty{sl}")
    out_res.extend([("yout", i) for i in range(4)])
    rec_target[0] = None
    est = emit_scheduled(blk_all)
    if SCHED_DEBUG:
        print("sched ops", len(blk_all), "est_us", round(est, 1))
    A("sp", None, out_res, ())

    P.plan()
    sems = {k: es.enter_context(nc.semaphore(str(k))) for k in P.semkeys}
    block = es.enter_context(nc.Block())

    @block.sync
    def _(e): P.emit_engine("sp", e, sems)

    @block.scalar
    def _(e): P.emit_engine("act", e, sems)

    @block.vector
    def _(e): P.emit_engine("dve", e, sems)

    @block.gpsimd
    def _(e): P.emit_engine("pool", e, sems)

    @block.tensor
    def _(e): P.emit_engine("pe", e, sems)

    es.close()
    return nc


def make_in_maps(inp):
    f = lambda a: np.ascontiguousarray(np.asarray(a, dtype=np.float32))
    shared = {
        "w_ada": f(inp["w_ada"]), "b_ada": f(inp["b_ada"]).reshape(DEPTH * 3, 1024), "w_in": f(inp["w_in"]),
        "hgrn_lb": f(inp["hgrn_lb"]), "hgrn_gnorm": f(inp["hgrn_gnorm"]), "conv_w": f(inp["conv_w"]).reshape(DEPTH * 3, 1024),
        "w_a_out": f(inp["w_a_out"]), "w_b_out": f(inp["w_b_out"]), "w_o": f(inp["w_o"]),
        "ln_g": f(inp["ln_g"]), "ln_b": f(inp["ln_b"]),
    }
    xp = f(inp["x_prompt"]); xs = f(inp["x_sample"]); sh = f(inp["state_hgrn"]); sc = f(inp["state_conv"])
    cpr = f(inp["c_prompt"]); csm = f(inp["c_sample"])
    maps = []
    for c in range(NCORES):
        s0, s1 = c * NSS, (c + 1) * NSS
        m = dict(shared)
        m["xp"] = xp[c]
        m["xs"] = np.ascontiguousarray(xs[s0:s1].reshape(NSS * TS, 1024))
        m["sh"] = np.ascontiguousarray(sh[:, s0:s1])
        m["scv"] = np.ascontiguousarray(sc[:, s0:s1].reshape(DEPTH * NSS * 2, 1024))
        m["cc"] = np.ascontiguousarray(np.concatenate([cpr[c:c + 1], csm[s0:s1]], axis=0))
        maps.append(m)
    return maps


def gather(results):
    y_prompt = np.stack([r["yp"] for r in results], axis=0)
    y_sample = np.concatenate([r["ys"].reshape(NSS, TS, 1024) for r in results], axis=0)
    nhp = np.stack([r["nhp"] for r in results], axis=1)
    ncp = np.stack([r["ncp"].reshape(DEPTH, 2, 1024) for r in results], axis=1)
    nhs = np.concatenate([r["nhs"] for r in results], axis=1)
    ncs = np.concatenate([r["ncs"].reshape(DEPTH, NSS, 2, 1024) for r in results], axis=1)
    return tuple(np.ascontiguousarray(a, dtype=np.float32) for a in (y_prompt, y_sample, nhp, ncp, nhs, ncs))


_NC = None


def kernel(**inputs):
    global _NC
    if _NC is None:
        _NC = build_nc()
    res = run_bass_kernel_spmd(_NC, make_in_maps(inputs), core_ids=list(range(NCORES)))
    return gather(res.results)
```

```python
import numpy as np
from contextlib import ExitStack
import concourse.bass as bass
import concourse.mybir as mybir
from concourse.bass_utils import run_bass_kernel_spmd

F32 = mybir.dt.float32
BF16 = mybir.dt.bfloat16
F32R = mybir.dt.float32r
I32 = mybir.dt.int32
AF = mybir.ActivationFunctionType
ALU = mybir.AluOpType

DEPTH = 4
NCORES = 8
SEQ = 2048
NSS = 16
TS = 4
NTOK = SEQ + NSS * TS
HALF_P = 1024
HALF_S = 32
HTOK = HALF_P + HALF_S
ALPHA = (2 * DEPTH) ** 0.25
LN_EPS = 1e-5
RMS_EPS = 1e-6
QSCALE = 128 ** -0.5
SCHED_DEBUG = False
SCHED_XLAT = 0.25
SCHED_WINDOW = 0.0
PE_SWITCH = 0.0


class _Op:
    __slots__ = ("id", "eng", "fn", "reads", "writes", "dma", "semkey", "deps", "signals", "sigidx", "grp")


class Prog:
    def __init__(self):
        self.ops = []
        self.last_writer = {}
        self.readers = {}

    def add(self, eng, fn, reads=(), writes=(), dma=False, semkey=None, grp=None):
        op = _Op()
        op.grp = grp
        if grp is not None:
            grp.append(op)
        op.id = len(self.ops); op.eng = eng; op.fn = fn
        op.reads = tuple(reads); op.writes = tuple(writes)
        op.dma = dma; op.semkey = semkey; op.signals = False; op.sigidx = None
        deps = set()
        for r in op.reads:
            w = self.last_writer.get(r)
            if w is not None:
                deps.add(w)
        for r in op.writes:
            w = self.last_writer.get(r)
            if w is not None:
                deps.add(w)
            for rd in self.readers.get(r, ()):
                deps.add(rd)
        deps.discard(op.id)
        op.deps = deps
        for r in op.reads:
            self.readers.setdefault(r, []).append(op.id)
        for r in op.writes:
            self.last_writer[r] = op.id
            self.readers[r] = []
        self.ops.append(op)
        return op

    def plan(self):
        ops = self.ops
        pos = {}; cnt = {}
        for op in ops:
            cnt[op.eng] = cnt.get(op.eng, 0) + 1
            pos[op.id] = cnt[op.eng]
        need = {}
        for op in ops:
            lst = []
            for d in op.deps:
                dop = ops[d]
                if dop.dma or dop.eng != op.eng:
                    lst.append(d)
                else:
                    if op.eng == "pe":
                        continue
                    lst.append(d)
            need[op.id] = lst
            for d in lst:
                ops[d].signals = True
        sigcnt = {}
        for op in ops:
            if op.dma:
                k = op.semkey
                sigcnt[k] = sigcnt.get(k, 0) + 16
                op.sigidx = sigcnt[k]
            elif op.signals:
                sigcnt[op.eng] = sigcnt.get(op.eng, 0) + 1
                op.sigidx = sigcnt[op.eng]
        self.sigcnt = sigcnt
        waited = {}
        streams = {}
        for op in ops:
            w = {}
            for d in need[op.id]:
                dop = ops[d]
                k = dop.semkey if dop.dma else dop.eng
                v = dop.sigidx if dop.grp is None else max(g.sigidx for g in dop.grp)
                if v > w.get(k, 0):
                    w[k] = v
            wl = []
            for k, v in w.items():
                if waited.get((op.eng, k), 0) < v:
                    waited[(op.eng, k)] = v
                    wl.append((k, v))
            streams.setdefault(op.eng, []).append((op, wl))
        self.streams = streams
        self.semkeys = sorted(set(sigcnt.keys()) | {"pe", "act", "dve", "pool"})

    def emit_engine(self, eng, e, sems):
        for op, wl in self.streams.get(eng, ()):
            for k, v in wl:
                e.wait_ge(sems[k], v)
            if op.fn is None:
                continue
            inst = op.fn(e)
            if op.dma:
                inst.then_inc(sems[op.semkey], 16)
            elif op.signals:
                inst.then_inc(sems[op.eng], 1)


def build_nc(depth=DEPTH):
    nc = bass.Bass("TRN2", target_bir_lowering=False)
    dt_in = lambda name, shape: nc.dram_tensor(name, list(shape), F32, kind="ExternalInput").ap()
    dt_out = lambda name, shape: nc.dram_tensor(name, list(shape), F32, kind="ExternalOutput").ap()
    xp = dt_in("xp", (SEQ, 1024)); xs = dt_in("xs", (NSS * TS, 1024))
    sh = dt_in("sh", (DEPTH, NSS, 8, 128, 128)); scv = dt_in("scv", (DEPTH * NSS * 2, 1024))
    cc = dt_in("cc", (1 + NSS, 1024))
    w_ada = dt_in("w_ada", (DEPTH, 1024, 3072)); b_ada = dt_in("b_ada", (DEPTH * 3, 1024))
    w_in = dt_in("w_in", (DEPTH, 1024, 10240))
    lbp = dt_in("hgrn_lb", (DEPTH, 1024)); gnp = dt_in("hgrn_gnorm", (DEPTH, 1024))
    cwp = dt_in("conv_w", (DEPTH * 3, 1024))
    w_a = dt_in("w_a_out", (DEPTH, 1024, 1024)); w_b = dt_in("w_b_out", (DEPTH, 1024, 1024))
    w_o = dt_in("w_o", (DEPTH, 1024, 1024))
    lngp = dt_in("ln_g", (DEPTH, 1024)); lnbp = dt_in("ln_b", (DEPTH, 1024))
    yp = dt_out("yp", (SEQ, 1024)); ys = dt_out("ys", (NSS * TS, 1024))
    nhp = dt_out("nhp", (DEPTH, 8, 128, 128)); ncp = dt_out("ncp", (DEPTH, 16, 128))
    nhs = dt_out("nhs", (DEPTH, NSS, 8, 128, 128)); ncs = dt_out("ncs", (DEPTH, 256, 128))

    es = ExitStack()
    sb = lambda name, shape, dt=F32: es.enter_context(nc.sbuf_tensor(name, list(shape), dt))
    X = sb("X", (128, 8, NTOK))
    H = sb("H", (128, 8, HTOK), BF16)
    AB = sb("AB", (128, 8, HTOK), BF16)
    M = sb("M", (128, 8, HTOK), BF16)
    WS = [sb(f"WS{i}", (128, 4096), BF16) for i in range(3)]
    Tall = sb("Tall", (128, 10 * 512))
    T = [[Tall[:, (p * 5 + i) * 512:(p * 5 + i + 1) * 512] for i in range(5)] for p in range(2)]
    B16all = sb("B16all", (128, 16 * 512), BF16)
    B16 = [[B16all[:, (p * 8 + i) * 512:(p * 8 + i + 1) * 512] for i in range(8)] for p in range(2)]
    U0 = sb("U0", (128, 516))
    U = [U0, U0]
    ATT = sb("ATT", (128, 4, 128), BF16)
    SBF = sb("SBF", (128, 8, 128), BF16)
    SR = [sb(f"SR{i}", (128, 128)) for i in range(3)]
    ATTS = sb("ATTS", (32, 32), BF16)
    S = sb("S", (128, 8, 128))
    KS2 = sb("KS2", (128, 256), BF16)
    S0 = [sb(f"S0_{i}", (128, 128)) for i in range(4)]
    SB0 = [sb(f"SB0_{i}", (128, 128), BF16) for i in range(4)]
    SN = [sb(f"SN_{i}", (128, 128)) for i in range(4)]
    MF = M[:].rearrange("p k n -> p (k n)").bitcast(F32)
    XIN = [MF[:, 2 * i * 528:(2 * i + 2) * 528].rearrange("p (g n) -> p g n", n=528)[:, :, 0:512] for i in range(4)]
    XKG = [[[("M", 2 * i + g, t) for t in range(3)] for g in range(2)] for i in range(4)]
    XK = [XKG[i][0] + XKG[i][1] for i in range(4)]
    IDF = sb("IDF", (128, 128)); IDB = sb("IDB", (128, 128), BF16)
    ONESF = sb("ONESF", (128, 128)); ONESB = sb("ONESB", (128, 128), BF16)
    ONE512 = sb("ONE512", (128, 512))
    CM = sb("CM", (128, 64)); CMS = sb("CMS", (32, 32)); SEL = sb("SEL", (32, 8))
    WF1 = WS[1][:].bitcast(F32); WF2 = WS[2][:].bitcast(F32)
    PRT = WF2[0:40, 1024:2048]
    PAR = sb("PAR", (128, 8, 40))
    LBT = sb("LBT", (128, 8, 16))
    LB = sb("LB", (128, 8, 4)); OML = sb("OML", (128, 8, 4)); NOML = sb("NOML", (128, 8, 4))
    CST = sb("CST", (128, 4))
    CCT = WF1[0:17, 0:1024]; CSB = sb("CSB", (128, 8, 17), BF16)
    SCT = WF2[:, 0:1024]; CH = sb("CH", (128, 8, 128))
    W1K = [("W", 1, i) for i in range(4)]; W2K = [("W", 2, i) for i in range(4)]
    ADA = sb("ADA", (128, 4, 8, 17))
    GCAR = sb("GCAR", (128, 8))
    PEV = [sb(f"PEV{i}", (128, 9)) for i in range(2)]
    SV = [sb(f"SV{i}", (128, 48)) for i in range(2)]
    SVE = [sb(f"SVE{i}", (128, 48)) for i in range(2)]
    UH = sb("UH", (128, 8, 2))
    US = sb("US", (128, 8, 6))
    COP = sb("COP", (128, 2, 8)); COS = sb("COS", (128, 16, 2, 8))
    COPT = T[1][4][0:16, 0:128]; COST = T[0][4][:, 0:256].rearrange("p (g n) -> p g n", n=128)
    PS = [es.enter_context(nc.psum_tensor(f"PS{i}", [128, 512], F32)) for i in range(7)]
    PSB = es.enter_context(nc.psum_tensor("PSB", [128, 1024], BF16))
    def pk(b):
        return [("ps", b, r) for r in range(4)]

    P = Prog()
    rec_target = [None]

    META = ("cost", "aset", "lat", "pmode")

    def A(eng, fn, reads=(), writes=(), **kw):
        if rec_target[0] is not None:
            rec_target[0].append((eng, fn, tuple(reads), tuple(writes), kw))
        else:
            P.add(eng, fn, reads, writes, **{k: v for k, v in kw.items() if k not in META})

    def fsz(ap):
        n = 1
        for d in ap.shape[1:]:
            n *= int(d)
        return n

    ASETS = {AF.Sigmoid: ("S",), AF.Silu: ("I",), AF.Ln: ("L",), AF.Exp: ("L", "E")}

    def emit_scheduled(blk):
        n = len(blk)
        lw, rds = {}, {}
        preds = [set() for _ in range(n)]
        for i, (eng, fn, R, W, kw) in enumerate(blk):
            for r in R:
                w = lw.get(r)
                if w is not None:
                    preds[i].add(w)
            for r in W:
                w = lw.get(r)
                if w is not None:
                    preds[i].add(w)
                preds[i].update(rds.get(r, ()))
            preds[i].discard(i)
            for r in R:
                rds.setdefault(r, []).append(i)
            for r in W:
                lw[r] = i
                rds[r] = []
        succs = [[] for _ in range(n)]
        indeg = [0] * n
        for i in range(n):
            indeg[i] = len(preds[i])
            for p in preds[i]:
                succs[p].append(i)
        cost = [blk[i][4].get("cost", 0.2) for i in range(n)]
        lat = [blk[i][4].get("lat", 0.0) for i in range(n)]
        prio = [0.0] * n
        for i in range(n - 1, -1, -1):
            m = 0.0
            for sx in succs[i]:
                if prio[sx] > m:
                    m = prio[sx]
            prio[i] = cost[i] + lat[i] + m
        etime = {}
        fin = [0.0] * n
        cur_set = [None]
        cur_pm = [None]
        ready = [i for i in range(n) if indeg[i] == 0]
        order = []
        while ready:
            cands = []
            tmin = None
            for i in ready:
                eng = blk[i][0]
                td = 0.0
                for p in preds[i]:
                    t = fin[p] + (0.05 if blk[p][0] == eng and not blk[p][4].get("dma") else SCHED_XLAT)
                    if t > td:
                        td = t
                t = max(etime.get(eng, 0.0), td)
                sw = 0.0
                aset = blk[i][4].get("aset")
                if aset is not None and cur_set[0] not in aset:
                    sw = 1.28
                pm = blk[i][4].get("pmode")
                if pm is not None and pm != cur_pm[0]:
                    sw = PE_SWITCH
                cands.append((t + sw, i, sw))
                if tmin is None or t + sw < tmin:
                    tmin = t + sw
            best = None
            for (t, i, sw) in cands:
                if t <= tmin + SCHED_WINDOW:
                    if best is None or prio[i] > prio[best[1]]:
                        best = (t, i, sw)
            t, i, sw = best
            eng = blk[i][0]
            aset = blk[i][4].get("aset")
            if aset is not None and cur_set[0] not in aset:
                cur_set[0] = aset[0]
            if blk[i][4].get("pmode") is not None:
                cur_pm[0] = blk[i][4]["pmode"]
            if blk[i][4].get("dma"):
                etime[eng] = t + cost[i]
                fin[i] = t + cost[i] + lat[i]
            else:
                etime[eng] = t + cost[i]
                fin[i] = t + cost[i]
            order.append(i)
            ready.remove(i)
            for sx in succs[i]:
                indeg[sx] -= 1
                if indeg[sx] == 0:
                    ready.append(sx)
        assert len(order) == n
        for i in order:
            eng, fn, R, W, kw = blk[i]
            P.add(eng, fn, R, W, **{k: v for k, v in kw.items() if k not in META})
        return max(fin) if n else 0.0

    def record(f, *args):
        lst = []
        rec_target[0] = lst
        f(*args)
        rec_target[0] = None
        return lst

    def replay(*lists):
        idx = [0] * len(lists)
        while True:
            best, bf = None, None
            for i, l in enumerate(lists):
                if idx[i] < len(l):
                    fr = idx[i] / len(l)
                    if bf is None or fr < bf:
                        best, bf = i, fr
            if best is None:
                break
            eng, fn, R, W, kw = lists[best][idx[best]]
            idx[best] += 1
            P.add(eng, fn, R, W, **kw)

    pending_tail = []

    def act(out, in_, func, R, W, bias=None, scale=None):
        kw = {}
        if bias is not None: kw["bias"] = bias
        if scale is not None: kw["scale"] = scale
        A("act", lambda e: e.activation(out=out, in_=in_, func=func, **kw), R, W,
          cost=0.2 + fsz(out) * 0.00078, aset=ASETS.get(func))

    def ecost(eng, ap, mult=1.0):
        n = fsz(ap)
        if eng == "pool":
            return 0.25 + n * 0.0016
        return 0.12 + n * 0.00105 * mult

    def tt(eng, out, in0, in1, op, R, W):
        A(eng, lambda e: e.tensor_tensor(out=out, in0=in0, in1=in1, op=op), R, W, cost=ecost(eng, out))

    def tsc(eng, out, in0, s1, s2, op0, op1, R, W):
        A(eng, lambda e: e.tensor_scalar(out=out, in0=in0, scalar1=s1, scalar2=s2, op0=op0, op1=op1), R, W, cost=ecost(eng, out))

    def stt(out, in0, scalar, in1, op0, op1, R, W):
        A("dve", lambda e: e.scalar_tensor_tensor(out=out, in0=in0, scalar=scalar, in1=in1, op0=op0, op1=op1), R, W,
          cost=ecost("dve", out, 1.15))

    def cp(eng, out, in_, R, W):
        if eng == "act":
            A("act", lambda e: e.copy(out=out, in_=in_), R, W, cost=0.2 + fsz(out) * 0.00078)
        else:
            A(eng, lambda e: e.tensor_copy(out=out, in_=in_), R, W, cost=ecost(eng, out))

    def r32(v):
        return 32 if v <= 32 else (64 if v <= 64 else 128)

    def mm(out, lhsT, rhs, start, stop, R, W):
        n = fsz(out)
        pmode = (r32(int(lhsT.shape[0])), r32(fsz(lhsT)))
        A("pe", lambda e: e.matmul(out=out, lhsT=lhsT, rhs=rhs, start=start, stop=stop), R, W,
          cost=(n * 0.00052 if n >= 256 else (0.115 if n >= 128 else 0.14)) * (4.0 if rhs.dtype == F32 else 1.0), pmode=pmode)

    def tr(out, in_, ident, R, W):
        A("pe", lambda e: e.transpose(out=out, in_=in_, identity=ident), R, W, cost=0.2,
          pmode=(r32(int(in_.shape[0])), r32(fsz(in_)), "T"))

    def dma(q, out, in_, R, W, semkey, grp=None, lat=2.5):
        A(q, lambda e: e.dma_start(out=out, in_=in_), R, W, dma=True, semkey=semkey, grp=grp,
          cost=(1.06 if q == "pool" else 0.06), lat=lat)

    def mset(eng, ap, val, W):
        A(eng, lambda e: e.memset(ap, val), (), W, cost=ecost(eng, ap))

    out_res = []
    blk_all = []
    rec_target[0] = blk_all

    mset("pool", IDF[:], 0.0, ["IDF"])
    A("pool", lambda e: e.affine_select(out=IDF[:], in_=IDF[:], pattern=[[-1, 128]], compare_op=ALU.not_equal,
                                         fill=1.0, base=0, channel_multiplier=1), ["IDF"], ["IDF"])
    cp("pool", IDB[:], IDF[:], ["IDF"], ["IDB"])
    mset("pool", ONESF[:], 1.0 / 1024.0, ["ONESF"])
    mset("pool", ONESB[:], 1.0, ["ONESB"])
    mset("pool", ONE512[:], 1.0, ["ONE512"])
    mset("pool", ATT[:], 0.0, [("ATT", 0), ("ATT", 1)])
    mset("pool", ATTS[:], 0.0, ["ATTS"])
    mset("pool", CST[:, 0:1], RMS_EPS, ["CST"])
    mset("pool", CST[:, 1:2], LN_EPS / (ALPHA * ALPHA), ["CST"])
    mset("pool", CM[:], 1.0, ["CM"])
    for hb in range(2):
        A("pool", lambda e, hb=hb: e.affine_select(out=CM[hb * 64:(hb + 1) * 64, :], in_=CM[hb * 64:(hb + 1) * 64, :],
                                                   pattern=[[1, 64]], compare_op=ALU.is_ge, fill=0.0, base=0,
                                                   channel_multiplier=-1), ["CM"], ["CM"])
    for hb in range(2):
        mset("pool", CM[hb * 64:hb * 64 + 32, 32:64], 0.0, ["CM"])
    mset("pool", SEL[:], 1.0, ["SEL"])
    A("pool", lambda e: e.affine_select(out=SEL[:], in_=SEL[:], pattern=[[-4, 8]], compare_op=ALU.is_ge, fill=0.0,
                                         base=0, channel_multiplier=1), ["SEL"], ["SEL"])
    A("pool", lambda e: e.affine_select(out=SEL[:], in_=SEL[:], pattern=[[4, 8]], compare_op=ALU.is_ge, fill=0.0,
                                         base=3, channel_multiplier=-1), ["SEL"], ["SEL"])
    cp("pool", CMS[:].rearrange("p (j t) -> p j t", t=4), SEL[:].unsqueeze(2).broadcast_to([32, 8, 4]), ["SEL"], ["CMS"])
    A("pool", lambda e: e.affine_select(out=CMS[:], in_=CMS[:], pattern=[[1, 32]], compare_op=ALU.is_ge, fill=0.0,
                                         base=0, channel_multiplier=-1), ["CMS"], ["CMS"])
    CMi = CM[:].bitcast(I32); CMSi = CMS[:].bitcast(I32)

    prm = [(lbp, 0, 4), (gnp, 4, 4), (cwp, 8, 12), (lngp, 20, 4), (lnbp, 24, 4), (b_ada, 28, 12)]
    for i, (src, r0, nr) in enumerate(prm):
        dma("sp", WF2[r0:r0 + nr, 1024:2048], src, [], W2K, f"ldp{i}")
    for k in range(8):
        tr(PS[0][:, k * 40:(k + 1) * 40], WF2[0:40, 1024 + k * 128:1024 + (k + 1) * 128], IDF[0:40, 0:40],
           W2K + ["IDF"], pk(0))
    cp("dve", PAR[:], PS[0][:, 0:320].rearrange("p (k r) -> p k r", r=40), pk(0), ["PAR"])
    tt("dve", LBT[:, :, 0:1], PAR[:, :, 0:1], PAR[:, :, 1:2], ALU.max, ["PAR"], ["LBT"])
    tt("dve", LBT[:, :, 1:2], PAR[:, :, 2:3], PAR[:, :, 3:4], ALU.max, ["PAR"], ["LBT1"])
    tt("dve", LBT[:, :, 0:1], LBT[:, :, 0:1], LBT[:, :, 1:2], ALU.max, ["LBT", "LBT1"], ["LBT"])
    tt("dve", LBT[:, :, 4:8], PAR[:, :, 0:4], LBT[:, :, 0:1].broadcast_to([128, 8, 4]), ALU.subtract, ["PAR", "LBT"], ["LBT2"])
    act(LBT[:, :, 8:12], LBT[:, :, 4:8], AF.Exp, ["LBT2"], ["LBT3"])
    tt("dve", LBT[:, :, 1:2], LBT[:, :, 8:9], LBT[:, :, 9:10], ALU.add, ["LBT3", "LBT1"], ["LBT1"])
    tt("dve", LBT[:, :, 2:3], LBT[:, :, 10:11], LBT[:, :, 11:12], ALU.add, ["LBT3"], ["LBT4"])
    tt("dve", LBT[:, :, 1:2], LBT[:, :, 1:2], LBT[:, :, 2:3], ALU.add, ["LBT1", "LBT4"], ["LBT1"])
    A("dve", lambda e: e.reciprocal(out=LBT[:, :, 3:4], in_=LBT[:, :, 1:2]), ["LBT1"], ["LBT5"])
    tt("dve", LBT[:, :, 12:16], LBT[:, :, 8:12], LBT[:, :, 3:4].broadcast_to([128, 8, 4]), ALU.mult, ["LBT3", "LBT5"], ["LBT6"])
    mset("dve", LB[:, :, 0:1], 0.0, ["LB0"])
    cp("dve", LB[:, :, 1:2], LBT[:, :, 13:14], ["LBT6"], ["LB1"])
    tt("dve", LB[:, :, 2:3], LB[:, :, 1:2], LBT[:, :, 14:15], ALU.add, ["LB1", "LBT6"], ["LB2"])
    tt("dve", LB[:, :, 3:4], LB[:, :, 2:3], LBT[:, :, 15:16], ALU.add, ["LB2", "LBT6"], ["LB3"])
    LBall = ["LB0", "LB1", "LB2", "LB3"]
    tsc("dve", OML[:], LB[:], -1.0, 1.0, ALU.mult, ALU.add, LBall, ["OML"])
    tsc("dve", NOML[:], LB[:], 1.0, -1.0, ALU.mult, ALU.add, LBall, ["NOML"])
    LBR = LBall + ["OML", "NOML"]

    dma("sp", CCT, cc, [], W1K, "ldc")
    for k in range(8):
        tr(PS[1][:, k * 17:(k + 1) * 17], WF1[0:17, k * 128:(k + 1) * 128], IDF[0:17, 0:17], W1K + ["IDF"], pk(1))
    act(CSB[:], PS[1][:, 0:136].rearrange("p (k r) -> p k r", r=17), AF.Silu, pk(1), ["CSB"])
    dma("sp", SCT, scv, [], W2K, "ldsc")
    for g in range(2):
        for k4 in range(4):
            k = g * 4 + k4
            tr(PS[2 + g][:, k4 * 128:(k4 + 1) * 128], WF2[:, k * 128:(k + 1) * 128], IDF[:], W2K + ["IDF"], pk(2 + g))
        cp("dve", CH[:, g * 4:(g + 1) * 4, :], PS[2 + g][:].rearrange("p (k r) -> p k r", r=128), pk(2 + g), [("CH", g)])
    CHR = [("CH", 0), ("CH", 1)]

    def xkeys(k, col0, n):
        return [("X", k, b) for b in range(col0 // 128, (col0 + n + 127) // 128)]

    nblk = SEQ // 128 + 1
    for b in range(nblk):
        sl = b % 4
        rows = 128 if b < SEQ // 128 else NSS * TS
        src = xp[b * 128:(b + 1) * 128, :] if b < SEQ // 128 else xs
        dma("sp", XIN[sl][0:rows, :, :], src.rearrange("p (g n) -> p g n", g=2), [], XK[sl], f"ldx{sl}")
        for g in range(2):
            bank = 5 + g if False else (2 + g)
            for k4 in range(4):
                k = g * 4 + k4
                tr(PS[bank][:, k4 * 128:k4 * 128 + rows], XIN[sl][0:rows, g, k4 * 128:(k4 + 1) * 128], IDF[0:rows, 0:rows],
                   XKG[sl][g] + ["IDF"], pk(bank))
            eng = "act" if g == 0 else "dve"
            cp(eng, X[:, g * 4:(g + 1) * 4, b * 128:b * 128 + rows],
               PS[bank][:].rearrange("p (k r) -> p k r", r=128)[:, :, 0:rows], pk(bank),
               [("X", k, b) for k in range(g * 4, g * 4 + 4)])

    stages = []

    def wload(slot, pieces):
        grp = []
        for i, (d, s) in enumerate(pieces):
            dma("pool", d, s, [], [("W", slot, i)], f"w{slot}", grp=grp, lat=7.0)

    def wkeys(slot, n):
        return [("W", slot, i) for i in range(n)]

    def wsrc(wt, l, c0, n):
        return wt[l, :, c0:c0 + n].rearrange("(k p) n -> p k n", p=128)

    def tiles_of(hf):
        return [("p", hf * HALF_P, 0, 512), ("p", hf * HALF_P + 512, 512, 512), ("s", SEQ + hf * HALF_S, HALF_P, HALF_S)]

    tcount = [0]

    def hkeys(name, k, lc, n):
        return [(name, k, lc // 512)]

    def ada_stage(l, g):
        def load(slot):
            wv = WS[slot][:].rearrange("p (k n) -> p k n", n=512)
            wload(slot, [(wv, wsrc(w_ada, l, g * 512, 512))])

        def comp(slot):
            wv = WS[slot][:].rearrange("p (k n) -> p k n", n=512)
            for mc in range(4):
                ec = g * 4 + mc
                part, chunk = ec // 8, ec % 8
                bank = mc % 2
                for k in range(8):
                    mm(PS[bank][:, 0:17], wv[:, k, mc * 128:(mc + 1) * 128], CSB[:, k, :], k == 0, k == 7,
                       wkeys(slot, 4) + ["CSB"], pk(bank))
                bcol = PAR[:, chunk, 28 + l * 3 + part:28 + l * 3 + part + 1]
                if part == 0:
                    tsc("dve", ADA[:, 0, chunk, :], PS[bank][:, 0:17], bcol, None, ALU.add, ALU.bypass, pk(bank) + ["PAR"], [("ADA", 0, chunk)])
                elif part == 1:
                    tsc("dve", ADA[:, 1, chunk, :], PS[bank][:, 0:17], bcol, 1.0, ALU.add, ALU.add, pk(bank) + ["PAR"], [("ADA", 1, chunk)])
                else:
                    tsc("dve", ADA[:, 2 + l % 2, chunk, :], PS[bank][:, 0:17], bcol, 1.0 / ALPHA, ALU.add, ALU.mult, pk(bank) + ["PAR"], [("ADA", 2 + l % 2, chunk)])
        return load, comp

    def h_stage(l, hf):
        def comp():
            for k in range(8):
                eng = "dve" if k % 2 == 0 else "pool"
                tsc(eng, H[:, k, 0:HALF_P], X[:, k, hf * HALF_P:(hf + 1) * HALF_P], ADA[:, 1, k, 0:1], ADA[:, 0, k, 0:1],
                    ALU.mult, ALU.add, xkeys(k, hf * HALF_P, HALF_P) + [("ADA", 1, k), ("ADA", 0, k)],
                    [("H", k, 0), ("H", k, 1)])
            xs_ = X[:, :, SEQ + hf * HALF_S:SEQ + (hf + 1) * HALF_S].rearrange("p k (s t) -> p k s t", t=TS)
            tmp = T[0][0][:, 0:256].rearrange("p (k s t) -> p k s t", s=8, t=TS)
            sc = ADA[:, 1, :, 1 + hf * 8:9 + hf * 8].unsqueeze(3).broadcast_to([128, 8, 8, TS])
            shf = ADA[:, 0, :, 1 + hf * 8:9 + hf * 8].unsqueeze(3).broadcast_to([128, 8, 8, TS])
            adk = [("ADA", pp, k) for pp in range(2) for k in range(8)]
            xk = [("X", k, 16) for k in range(8)]
            tt("dve", tmp, xs_, sc, ALU.mult, xk + adk, [("T", 0, 0)])
            tt("dve", H[:, :, HALF_P:HTOK].rearrange("p k (s t) -> p k s t", t=TS), tmp, shf, ALU.add,
               [("T", 0, 0)] + adk, [("H", k, 2) for k in range(8)])
        return comp

    acount = [0]

    def a_stage(l, hf, j):
        def load(slot):
            wv = WS[slot][:].rearrange("p (k b n) -> p k b n", b=4, n=128)
            wload(slot, [(wv[:, :, b, :], wsrc(w_in, l, b * 1024 + j * 128, 128)) for b in range(4)])

        def comp(slot):
            wv = WS[slot][:].rearrange("p (k b n) -> p k b n", b=4, n=128)
            WK = wkeys(slot, 4)
            lbv = LB[:, j, l:l + 1]; omlv = OML[:, j, l:l + 1]; nomlv = NOML[:, j, l:l + 1]
            gnv = PAR[:, j, 4 + l:5 + l]
            Sj = S[:, j, :]
            if hf == 0:
                mset("pool", Sj, 0.0, [("S", j)])
                mset("pool", GCAR[:, j:j + 1], 0.0, [("GCAR", j)])
            cx = []
            for ti, (kind, xc0, lc0, n) in enumerate(tiles_of(hf)):
                pb = acount[0] % 2
                acount[0] += 1
                cx.append((ti, kind, xc0, lc0, n, pb))

            def unpack(c):
                ti, kind, xc0, lc0, n, pb = cx[c]
                Tt = T[0]
                QS, KS, KDT, KD, VT, SG, OSQ, QI = B16[pb]
                TK = lambda i: ("T", 0, i)
                BK = lambda i: ("B", pb, i)
                nst = max(1, n // 128)
                rows = 128 if kind == "p" else n
                return ti, kind, xc0, lc0, n, pb, Tt, QS, KS, KDT, KD, VT, SG, OSQ, QI, TK, BK, nst, rows

            def inproj(c):
                ti, kind, xc0, lc0, n, pb, Tt, QS, KS, KDT, KD, VT, SG, OSQ, QI, TK, BK, nst, rows = unpack(c)
                HK = [("H", k, ti) for k in range(8)]
                for blk, bank in ((1, 1), (0, 0), (3, 2)):
                    for k in range(8):
                        mm(PS[bank][:, 0:n], wv[:, k, blk, :], H[:, k, lc0:lc0 + n], k == 0, k == 7, WK + [HK[k]], pk(bank))
                if kind == "p":
                    for k in range(8):
                        mm(PS[3][:, 0:n], wv[:, k, 2, :], H[:, k, lc0:lc0 + n], k == 0, k == 7, WK + [HK[k]], pk(3))
                else:
                    for k in range(8):
                        mm(PS[3][0:rows, 0:128], H[:, k, lc0:lc0 + rows], wv[:, k, 2, :], k == 0, k == 7, WK + [HK[k]], pk(3))

            def aphase(c):
                ti, kind, xc0, lc0, n, pb, Tt, QS, KS, KDT, KD, VT, SG, OSQ, QI, TK, BK, nst, rows = unpack(c)
                act(Tt[0][:, 0:n], PS[1][:, 0:n], AF.Sigmoid, pk(1), [TK(0)])
                act(Tt[4][:, 0:n], PS[0][:, 0:n], AF.Sigmoid, pk(0), [TK(4)])
                act(Tt[1][:, 0:n], PS[2][:, 0:n], AF.Sigmoid, pk(2), [TK(1)])
                if kind == "p":
                    cp("act", OSQ[:, 0:512], PS[3][:, 0:512], pk(3), [BK(6)])
                    for st in range(4):
                        tr(PSB[:, 512 + st * 128:512 + (st + 1) * 128], OSQ[:, st * 128:(st + 1) * 128], IDB[:], [BK(6), "IDB"], ["psb"])
                    cp("dve", VT[:, 0:512], PSB[:, 512:1024], ["psb"], [BK(4)])
                else:
                    cp("dve", VT[0:rows, 0:nst * 128], PS[3][0:rows, 0:nst * 128], pk(3), [BK(4)])
                tt("dve", Tt[4][:, 0:n], PS[0][:, 0:n], Tt[4][:, 0:n], ALU.mult, pk(0) + [TK(4)], [TK(4)])
                tt("dve", SG[:, 0:n], PS[2][:, 0:n], Tt[1][:, 0:n], ALU.mult, pk(2) + [TK(1)], [BK(5)])
                tsc("pool", Tt[3][:, 0:n], Tt[0][:, 0:n], nomlv, omlv, ALU.mult, ALU.add, [TK(0)] + LBR, [TK(3)])
                act(Tt[0][:, 0:n], Tt[0][:, 0:n], AF.Ln, [TK(0)] + LBR, [TK(0)], bias=lbv, scale=omlv)
                if kind == "p":
                    A("dve", lambda e, Tt=Tt, j=j: e.tensor_tensor_scan(out=Tt[1][:, 0:512], data0=ONE512[:], data1=Tt[0][:, 0:512],
                                                                        initial=GCAR[:, j:j + 1], op0=ALU.mult, op1=ALU.add),
                      [TK(0), "ONE512", ("GCAR", j)], [TK(1)], cost=1.3)
                    G3 = Tt[1][:].rearrange("p (c t) -> p c t", t=64)
                    G3h = Tt[1][:].rearrange("p (h t) -> p h t", t=32)
                    mid82 = Tt[1][:].rearrange("p (c h t) -> p c h t", h=2, t=32)[:, :, :, 15]
                    cp("dve", PEV[pb][:, 0:1], GCAR[:, j:j + 1], [("GCAR", j)], [("PEV", pb)])
                    cp("dve", PEV[pb][:, 1:9], G3[:, :, 63], [TK(1)], [("PEV", pb)])
                    cp("dve", GCAR[:, j:j + 1], Tt[1][:, 511:512], [TK(1)], [("GCAR", j)])
                    tt("dve", SV[pb][:, 0:16].rearrange("p (c h) -> p c h", h=2), G3[:, :, 63:64].broadcast_to([128, 8, 2]), mid82,
                       ALU.subtract, [TK(1)], [("SV", pb)])
                    tt("dve", SV[pb][:, 16:24], PEV[pb][:, 1:9], PEV[pb][:, 0:8], ALU.subtract, [("PEV", pb)], [("SV", pb)])
                    tt("dve", SV[pb][:, 24:40].rearrange("p (c h) -> p c h", h=2), mid82,
                       PEV[pb][:, 0:8].unsqueeze(2).broadcast_to([128, 8, 2]), ALU.subtract, [TK(1), ("PEV", pb)], [("SV", pb)])
                    tt("dve", SV[pb][:, 40:48], mid82[:, :, 1], mid82[:, :, 0], ALU.subtract, [TK(1)], [("SV", pb)])
                    act(SVE[pb][:], SV[pb][:], AF.Exp, [("SV", pb)], [("SVE", pb)])
                    tt("dve", Tt[2][:].rearrange("p (h t) -> p h t", t=32), G3h, G3h[:, :, 15:16].broadcast_to([128, 16, 32]),
                       ALU.subtract, [TK(1)], [TK(2)])
                    act(Tt[1][:], Tt[2][:], AF.Exp, [TK(2)], [TK(1)])
                    act(Tt[2][:], Tt[2][:], AF.Exp, [TK(2)], [TK(2)], scale=-1.0)
                    EQ, EK = Tt[1], Tt[2]
                    ccb = SVE[pb][:, 0:16].unsqueeze(2).broadcast_to([128, 16, 32])
                    v3 = lambda ap: ap[:, 0:512].rearrange("p (h t) -> p h t", t=32)
                else:
                    L3 = Tt[0][:, 0:n].rearrange("p (s t) -> p s t", t=TS)
                    G3 = Tt[1][:, 0:n].rearrange("p (s t) -> p s t", t=TS)
                    cp("dve", G3[:, :, 0:1], L3[:, :, 0:1], [TK(0)], [TK(1)])
                    for t in range(1, TS):
                        tt("dve", G3[:, :, t:t + 1], G3[:, :, t - 1:t], L3[:, :, t:t + 1], ALU.add, [TK(0), TK(1)], [TK(1)])
                    act(Tt[2][:, 0:n], Tt[1][:, 0:n], AF.Exp, [TK(1)], [TK(2)])
                    act(Tt[1][:, 0:n], Tt[1][:, 0:n], AF.Exp, [TK(1)], [TK(1)], scale=-1.0)
                    EQ, EK = Tt[2], Tt[1]
                    E3 = Tt[2][:, 0:n].rearrange("p (s t) -> p s t", t=TS)
                    cp("dve", SVE[pb][:, 0:8], E3[:, :, TS - 1], [TK(2)], [("SVE", pb)])
                    ccb = SVE[pb][:, 0:8].unsqueeze(2).broadcast_to([128, 8, TS])
                    v3 = lambda ap: ap[:, 0:n].rearrange("p (s t) -> p s t", t=TS)
                stt(QS[:, 0:n], Tt[4][:, 0:n], QSCALE, EQ[:, 0:n], ALU.mult, ALU.mult, [TK(4), TK(1), TK(2)], [BK(0)])
                tt("dve", KS[:, 0:n], Tt[3][:, 0:n], EK[:, 0:n], ALU.mult, [TK(3), TK(1), TK(2)], [BK(1)])
                tt("pool", v3(KDT), v3(KS), ccb, ALU.mult, [BK(1), ("SVE", pb)], [BK(2)])
                if kind == "p":
                    tt("pool", KS2[:].rearrange("p (c t) -> p c t", t=32), KS[:].rearrange("p (c t) -> p c t", t=64)[:, :, 0:32],
                       SVE[pb][:, 40:48].unsqueeze(2).broadcast_to([128, 8, 32]), ALU.mult, [BK(1), ("SVE", pb)], ["KS2"])
                    tt("pool", v3(QI), v3(QS), SVE[pb][:, 24:40].unsqueeze(2).broadcast_to([128, 16, 32]), ALU.mult,
                       [BK(0), ("SVE", pb)], [BK(7)])

            def recA(c):
                ti, kind, xc0, lc0, n, pb, Tt, QS, KS, KDT, KD, VT, SG, OSQ, QI, TK, BK, nst, rows = unpack(c)
                KDv = KD[:].rearrange("p (s n) -> p s n", n=128)
                VTv = VT[:].rearrange("p (s n) -> p s n", n=128)
                for st in range(nst):
                    tr(PSB[0:rows, st * 128:(st + 1) * 128], KDT[:, st * 128:st * 128 + rows], IDB[:], [BK(2), "IDB"], ["psb"])
                cp("act", KD[0:rows, 0:nst * 128], PSB[0:rows, 0:nst * 128], ["psb"], [BK(3)])
                def supd(ch):
                    hb, st = ch % 2, ch // 2
                    p0 = hb * 64
                    xr = [("RT2", pb)] if hb == 1 else []
                    xw = [("RT1", pb)] if hb == 0 else []
                    mm(PS[5][:, (ch % 4) * 128:(ch % 4) * 128 + 128], KDv[p0:p0 + 64, st, :], VTv[p0:p0 + 64, st, :], True, True,
                       [BK(3), BK(4)] + xr, pk(5) + xw)
                if kind == "p":
                    supd(0); supd(2)
                    for ch in range(8):
                        hb, cc = ch % 2, ch // 2
                        p0 = hb * 64
                        cs = slice(ch * 64, (ch + 1) * 64)
                        csB = slice(ch * 64 + 32, ch * 64 + 64)
                        mm(PS[4][p0:p0 + 64, cc * 128:cc * 128 + 64], KS[:, cs], QS[:, cs], True, True, [BK(0), BK(1)], [("ps", 4, hb)])
                        mm(PS[4][p0:p0 + 32, cc * 128 + 64:cc * 128 + 96], KS2[:, ch * 32:(ch + 1) * 32], QS[:, csB], True, True,
                           [BK(0), "KS2"] + ([("RT1", pb)] if ch == 7 else []), [("ps", 4, hb)] + ([("RT2", pb)] if ch == 7 else []))
                    for hb in range(2):
                        p0 = hb * 64
                        A("dve", lambda e, p0=p0: e.copy_predicated(
                            out=ATT[p0:p0 + 64, :, p0:p0 + 64], mask=CMi[p0:p0 + 64, :].unsqueeze(1).broadcast_to([64, 4, 64]),
                            data=PS[4][p0:p0 + 64, :].rearrange("p (c t) -> p c t", t=128)[:, :, 0:64]),
                          [("ps", 4, hb), "CM"], [("ATT", hb)], cost=0.45)
                        cp("act", ATT[p0:p0 + 32, :, p0 + 32:p0 + 64], PS[4][p0:p0 + 32, :].rearrange("p (c t) -> p c t", t=128)[:, :, 64:96],
                           [("ps", 4, hb)], [("ATT", hb)])
                    supd(1); supd(3)
                else:
                    mm(PS[4][0:n, 0:n], KS[:, 0:n], QS[:, 0:n], True, True, [BK(0), BK(1)], [("ps", 4, 0)])
                    A("dve", lambda e, n=n: e.copy_predicated(out=ATTS[:], mask=CMSi, data=PS[4][0:n, 0:n]), [("ps", 4, 0), "CMS"], ["ATTS"])
                    VM = B16all[0:n, (pb * 8 + 6) * 512:(pb * 8 + 8) * 512].rearrange("p (s v) -> p s v", v=128)
                    tt("dve", VM, VTv[0:n, 0, :].unsqueeze(1).broadcast_to([n, 8, 128]), SEL[:].unsqueeze(2).broadcast_to([n, 8, 128]),
                       ALU.mult, [BK(4), "SEL"], [BK(6), BK(7)])

            def chain(c):
                ti, kind, xc0, lc0, n, pb, Tt, QS, KS, KDT, KD, VT, SG, OSQ, QI, TK, BK, nst, rows = unpack(c)
                KDv = KD[:].rearrange("p (s n) -> p s n", n=128)
                VTv = VT[:].rearrange("p (s n) -> p s n", n=128)
                def supd2(c2):
                    hb2, st2 = c2 % 2, c2 // 2
                    q0 = hb2 * 64
                    xr = [("RT4", pb)] if hb2 == 1 else []
                    xw = [("RT3", pb)] if hb2 == 0 else []
                    mm(PS[5][:, (c2 % 4) * 128:(c2 % 4) * 128 + 128], KDv[q0:q0 + 64, st2, :], VTv[q0:q0 + 64, st2, :],
                       True, True, [BK(3), BK(4)] + xr, pk(5) + xw)
                if kind == "p":
                    for ch in range(8):
                        hb, st = ch % 2, ch // 2
                        p0 = hb * 64
                        cs = slice(ch * 64, (ch + 1) * 64)
                        sprev, kprev = (Sj, ("S", j)) if ch == 0 else (SR[ch % 3][:], ("SR", ch % 3))
                        snext, knext = (Sj, ("S", j)) if ch == 7 else (SR[(ch + 1) % 3][:], ("SR", (ch + 1) % 3))
                        cp("act", SBF[:, ch, :], sprev, [kprev], [("SBF", ch)])
                        if hb == 0:
                            mm(PS[6][:, st * 128:(st + 1) * 128], VTv[:, st, :], ATT[:, st, :], True, False,
                               [BK(4), ("ATT", 0), ("ATT", 1)], pk(6))
                        mm(PS[6][:, cs], SBF[:, ch, :], QI[:, cs], False, hb == 1,
                           [("SBF", ch), BK(7)] + ([("RT3", pb)] if ch == 4 else []), pk(6) + ([("RT4", pb)] if ch == 4 else []))
                        stt(snext, sprev, SVE[pb][:, 16 + ch:17 + ch], PS[5][:, (ch % 4) * 128:(ch % 4) * 128 + 128], ALU.mult, ALU.add,
                            [kprev, ("SVE", pb), ("ps", 5, ch % 4)], [knext])
                        if ch == 3:
                            supd2(4); supd2(6)
                        if ch == 4:
                            supd2(5); supd2(7)
                    if hf == 1 and ti == 1:
                        dma("sp", nhp[l, j], Sj, [("S", j)], [("nhp", l, j)], f"sthp{j}")
                        out_res.append(("nhp", l, j))
                else:
                    VM = B16all[0:n, (pb * 8 + 6) * 512:(pb * 8 + 8) * 512].rearrange("p (s v) -> p s v", v=128)
                    mm(PS[6][:, 0:n], VTv[0:n, 0, :], ATTS[:], True, False, [BK(4), "ATTS"], pk(6))
                    for sb4 in range(2):
                        for sq in range(sb4 * 4, sb4 * 4 + 4):
                            gs = hf * 8 + sq
                            sl = sq % 4
                            if sb4 == 1:
                                dma("sp", S0[sl][:], sh[l, gs, j], [], [("S0", sl)], f"lds{sl}")
                            cp("pool", SB0[sl][:], S0[sl][:], [("S0", sl)], [("SB0", sl)])
                            mm(PS[6][:, sq * TS:(sq + 1) * TS], SB0[sl][:], QS[:, sq * TS:(sq + 1) * TS], False, sq == 7,
                               [("SB0", sl), BK(0)], pk(6))
                        for sq in range(sb4 * 4, sb4 * 4 + 4):
                            r = sq % 4
                            mm(PS[5][:, r * 128:r * 128 + 128], KDv[0:n, 0, :], VM[:, sq, :], True, True, [BK(3), BK(6), BK(7)], pk(5))
                        for sq in range(sb4 * 4, sb4 * 4 + 4):
                            gs = hf * 8 + sq
                            sl = sq % 4
                            r = sq % 4
                            stt(SN[sl][:], S0[sl][:], SVE[pb][:, sq:sq + 1], PS[5][:, r * 128:r * 128 + 128], ALU.mult, ALU.add,
                                [("S0", sl), ("SVE", pb), ("ps", 5, r)], [("SN", sl)])
                            dma("sp", nhs[l, gs, j], SN[sl][:], [("SN", sl)], [("nhs", sl)], f"stsn{sl}")
                            if ("nhs", sl) not in out_res:
                                out_res.append(("nhs", sl))

            def norm(c):
                ti, kind, xc0, lc0, n, pb, Tt, QS, KS, KDT, KD, VT, SG, OSQ, QI, TK, BK, nst, rows = unpack(c)
                RSn = B16all[:, (pb * 8 + 0) * 512:(pb * 8 + 2) * 512].bitcast(F32)
                T1n = B16all[:, (pb * 8 + 2) * 512:(pb * 8 + 4) * 512].bitcast(F32)
                K01 = [BK(0), BK(1)]; K23 = [BK(2), BK(3)]
                act(OSQ[:, 0:n], PS[6][:, 0:n], AF.Square, pk(6), [BK(6)])
                mm(PS[4][:, 0:n], ONESB[:], OSQ[:, 0:n], True, True, [BK(6), "ONESB"], pk(4))
                act(RSn[:, 0:n], PS[4][:, 0:n], AF.Ln, pk(4) + ["CST"], K01, bias=CST[:, 0:1], scale=1.0 / 128.0)
                act(RSn[:, 0:n], RSn[:, 0:n], AF.Exp, K01, K01, scale=-0.5)
                stt(T1n[:, 0:n], PS[6][:, 0:n], gnv, RSn[:, 0:n], ALU.mult, ALU.mult, pk(6) + ["PAR"] + K01, K23)
                tt("dve", AB[:, j, lc0:lc0 + n], T1n[:, 0:n], SG[:, 0:n], ALU.mult, K23 + [BK(5)], [("AB", j, ti)])

            for sq in range(4):
                dma("sp", S0[sq][:], sh[l, hf * 8 + sq, j], [], [("S0", sq)], f"lds{sq}")
            for c in range(3):
                inproj(c); aphase(c); recA(c); chain(c); norm(c)
        return load, comp

    mbank = [0]

    def merge_stage(l, hf, i, which):
        wsrc_y = w_a if which == 0 else w_b
        rcol = 8192 + which * 1024 + i * 128

        def load(slot):
            wv = WS[slot][:, 0:2048].rearrange("p (k b n) -> p k b n", b=2, n=128)
            wload(slot, [(wv[:, :, 0, :], wsrc(wsrc_y, l, i * 128, 128)), (wv[:, :, 1, :], wsrc(w_in, l, rcol, 128))])

        def comp(slot):
            wv = WS[slot][:, 0:2048].rearrange("p (k b n) -> p k b n", b=2, n=128)
            WK = wkeys(slot, 4)
            for ti, (kind, xc0, lc0, n) in enumerate(tiles_of(hf)):
                pb = tcount[0] % 2
                tcount[0] += 1
                Tt = T[pb]
                TK = lambda ii: ("T", pb, ii)
                ba, bb = ((0, 1), (2, 3))[mbank[0] % 2]
                mbank[0] += 1
                for k in range(8):
                    mm(PS[ba][:, 0:n], wv[:, k, 0, :], AB[:, k, lc0:lc0 + n], k == 0, k == 7, WK + [("AB", k, ti)], pk(ba))
                for k in range(8):
                    mm(PS[bb][:, 0:n], wv[:, k, 1, :], H[:, k, lc0:lc0 + n], k == 0, k == 7, WK + [("H", k, ti)], pk(bb))
                act(Tt[0][:, 0:n], PS[bb][:, 0:n], AF.Sigmoid, pk(bb), [TK(0)])
                if which == 0:
                    tt("dve", M[:, i, lc0:lc0 + n], PS[ba][:, 0:n], Tt[0][:, 0:n], ALU.mult, pk(ba) + [TK(0)], [("M", i, ti)])
                else:
                    tt("dve", Tt[1][:, 0:n], PS[ba][:, 0:n], Tt[0][:, 0:n], ALU.mult, pk(ba) + [TK(0)], [TK(1)])
                    tt("pool", M[:, i, lc0:lc0 + n], M[:, i, lc0:lc0 + n], Tt[1][:, 0:n], ALU.add, [("M", i, ti), TK(1)], [("M", i, ti)])
        return load, comp

    def b_stage(l, hf, j):
        def load(slot):
            wv = WS[slot][:].rearrange("p (k b n) -> p k b n", b=4, n=128)
            wload(slot, [(wv[:, :, b, :], wsrc(w_in, l, 4096 + b * 1024 + j * 128, 128)) for b in range(4)])

        def comp(slot):
            wv = WS[slot][:].rearrange("p (k b n) -> p k b n", b=4, n=128)
            WK = wkeys(slot, 4)
            w0 = PAR[:, j, 8 + l * 3:9 + l * 3]; w1 = PAR[:, j, 9 + l * 3:10 + l * 3]; w2 = PAR[:, j, 10 + l * 3:11 + l * 3]
            if hf == 0:
                mset("pool", UH[:, j, :], 0.0, [("UH", j)])
            for ti, (kind, xc0, lc0, n) in enumerate(tiles_of(hf)):
                pb = 0
                Tt = T[1]
                TK = lambda ii: ("T", 1, ii)
                HK = [("H", k, ti) for k in range(8)]
                for blk in (2, 1, 3, 0):
                    for k in range(8):
                        mm(PS[blk][:, 0:n], wv[:, k, blk, :], H[:, k, lc0:lc0 + n], k == 0, k == 7, WK + [HK[k]], pk(blk))
                cp("act", Tt[0][:, 0:n], PS[2][:, 0:n], pk(2), [TK(0)])
                act(Tt[2][:, 0:n], PS[3][:, 0:n], AF.Sigmoid, pk(3), [TK(2)])
                tt("dve", Tt[2][:, 0:n], PS[3][:, 0:n], Tt[2][:, 0:n], ALU.mult, pk(3) + [TK(2)], [TK(2)])
                if kind == "p":
                    Ut = U[pb]
                    cp("pool", Ut[:, 0:2], UH[:, j, :], [("UH", j)], [("U", 0)])
                    tt("dve", Ut[:, 2:2 + n], PS[1][:, 0:n], Tt[0][:, 0:n], ALU.mult, pk(1) + [TK(0)], [("U", 0)])
                    cp("pool", UH[:, j, :], Ut[:, n:n + 2], [("U", 0)], [("UH", j)])
                    if hf == 1 and ti == 1:
                        cp("pool", COP[:, :, j], Ut[:, n:n + 2], [("U", 0)], [("COP", j)])
                    tsc("dve", Tt[1][:, 0:n], Ut[:, 0:n], w0, None, ALU.mult, ALU.bypass, [("U", 0), "PAR"], [TK(1)])
                    stt(Tt[1][:, 0:n], Ut[:, 1:n + 1], w1, Tt[1][:, 0:n], ALU.mult, ALU.add, [("U", 0), "PAR", TK(1)], [TK(1)])
                    stt(Tt[1][:, 0:n], Ut[:, 2:n + 2], w2, Tt[1][:, 0:n], ALU.mult, ALU.add, [("U", 0), "PAR", TK(1)], [TK(1)])
                else:
                    r0 = l * 32 + hf * 16
                    cp("pool", US[:, :, 0:2], CH[:, j, r0:r0 + 16].rearrange("p (s t) -> p s t", t=2), CHR, ["US"])
                    tt("dve", US[:, :, 2:6], PS[1][:, 0:n].rearrange("p (s t) -> p s t", t=TS),
                       Tt[0][:, 0:n].rearrange("p (s t) -> p s t", t=TS), ALU.mult, pk(1) + [TK(0)], ["US"])
                    cp("pool", COS[:, hf * 8:(hf + 1) * 8, :, j], US[:, :, 4:6], ["US"], [("COS", j)])
                    t3 = Tt[1][:, 0:n].rearrange("p (s t) -> p s t", t=TS)
                    tsc("dve", t3, US[:, :, 0:4], w0, None, ALU.mult, ALU.bypass, ["US", "PAR"], [TK(1)])
                    stt(t3, US[:, :, 1:5], w1, t3, ALU.mult, ALU.add, ["US", "PAR", TK(1)], [TK(1)])
                    stt(t3, US[:, :, 2:6], w2, t3, ALU.mult, ALU.add, ["US", "PAR", TK(1)], [TK(1)])
                tt("dve", Tt[3][:, 0:n], PS[0][:, 0:n], Tt[1][:, 0:n], ALU.mult, pk(0) + [TK(1)], [TK(3)])
                tt("pool", AB[:, j, lc0:lc0 + n], Tt[3][:, 0:n], Tt[2][:, 0:n], ALU.mult, [TK(3), TK(2)], [("AB", j, ti)])
        return load, comp

    def conv_out(l):
        def comp():
            tr(PS[0][0:16, 0:128], COP[:].rearrange("p t j -> p (t j)"), IDF[:], [("COP", j) for j in range(8)] + ["IDF"], pk(0))
            cp("dve", COPT, PS[0][0:16, 0:128], pk(0), [("T", 1, 4)])
            dma("sp", ncp[l], COPT, [("T", 1, 4)], ["ncp"], "stcp")
            cosf = COS[:].rearrange("p s t j -> p (s t j)")
            for g in range(2):
                tr(PS[1][:, g * 128:(g + 1) * 128], cosf[:, g * 128:(g + 1) * 128], IDF[:], [("COS", j) for j in range(8)] + ["IDF"], pk(1))
            cp("dve", COST, PS[1][:, 0:256].rearrange("p (g n) -> p g n", n=128), pk(1), [("T", 0, 4)])
            dma("sp", ncs[l].rearrange("(g r) n -> r g n", g=2), COST, [("T", 0, 4)], ["ncs"], "stcs")
        return comp
    out_res.extend(["ncp", "ncs"])

    def o_stage(l, hf, i):
        def load(slot):
            wv = WS[slot][:, 0:1024].rearrange("p (k n) -> p k n", n=128)
            wload(slot, [(wv, wsrc(w_o, l, i * 128, 128))])

        def comp(slot):
            wv = WS[slot][:, 0:1024].rearrange("p (k n) -> p k n", n=128)
            WK = wkeys(slot, 4)
            for ti, (kind, xc0, lc0, n) in enumerate(tiles_of(hf)):
                pb = tcount[0] % 2
                tcount[0] += 1
                Tt = T[pb]
                TK = lambda ii: ("T", pb, ii)
                ba = (0, 1, 2, 3)[mbank[0] % 4]
                mbank[0] += 1
                for k in range(8):
                    mm(PS[ba][:, 0:n], wv[:, k, :], M[:, k, lc0:lc0 + n], k == 0, k == 7, WK + [("M", k, ti)], pk(ba))
                xk = xkeys(i, xc0, n)
                if kind == "p":
                    stt(X[:, i, xc0:xc0 + n], PS[ba][:, 0:n], ADA[:, 2 + l % 2, i, 0:1], X[:, i, xc0:xc0 + n], ALU.mult, ALU.add,
                        pk(ba) + [("ADA", 2 + l % 2, i)] + xk, xk)
                else:
                    gt = ADA[:, 2 + l % 2, i, 1 + hf * 8:9 + hf * 8].unsqueeze(2).broadcast_to([128, 8, TS])
                    t3 = Tt[0][:, 0:n].rearrange("p (s t) -> p s t", t=TS)
                    tt("dve", t3, PS[ba][:, 0:n].rearrange("p (s t) -> p s t", t=TS), gt, ALU.mult, pk(ba) + [("ADA", 2 + l % 2, i)], [TK(0)])
                    tt("dve", X[:, i, xc0:xc0 + n], X[:, i, xc0:xc0 + n], Tt[0][:, 0:n], ALU.add, [TK(0)] + xk, xk)
        return load, comp

    def ln_stage(l, hf):
        def comp():
            for ti, (kind, xc0, lc0, n) in enumerate(tiles_of(hf)):
                pb = tcount[0] % 2
                tcount[0] += 1
                Tt = T[pb]
                TK = lambda ii: ("T", pb, ii)
                for k in range(8):
                    mm(PS[0][:, 0:n], ONESF[:], X[:, k, xc0:xc0 + n], k == 0, k == 7, ["ONESF"] + xkeys(k, xc0, n), pk(0))
                for k in range(8):
                    sq = Tt[3 + k % 2].bitcast(BF16)
                    act(sq[:, 0:n], X[:, k, xc0:xc0 + n], AF.Square, xkeys(k, xc0, n), [TK(3 + k % 2)])
                    mm(PS[1][:, 0:n], ONESB[:], sq[:, 0:n], k == 0, k == 7, ["ONESB", TK(3 + k % 2)], pk(1))
                cp("act", Tt[0][:, 0:n], PS[0][:, 0:n], pk(0), [TK(0)])
                stt(Tt[1][:, 0:n], Tt[0][:, 0:n], -1.0, Tt[0][:, 0:n], ALU.mult, ALU.mult, [TK(0)], [TK(1)])
                stt(Tt[1][:, 0:n], PS[1][:, 0:n], 1.0 / 1024.0, Tt[1][:, 0:n], ALU.mult, ALU.add, [TK(1)] + pk(1), [TK(1)])
                act(Tt[1][:, 0:n], Tt[1][:, 0:n], AF.Ln, [TK(1), "CST"], [TK(1)], bias=CST[:, 1:2], scale=1.0)
                act(Tt[1][:, 0:n], Tt[1][:, 0:n], AF.Exp, [TK(1)], [TK(1)], scale=-0.5)
                for k in range(8):
                    tm = Tt[3 + k % 2]
                    xk = xkeys(k, xc0, n)
                    tt("pool", tm[:, 0:n], X[:, k, xc0:xc0 + n], Tt[0][:, 0:n], ALU.subtract, xk + [TK(0)], [TK(3 + k % 2)])
                    tt("dve", tm[:, 0:n], tm[:, 0:n], Tt[1][:, 0:n], ALU.mult, [TK(3 + k % 2), TK(1)], [TK(3 + k % 2)])
                    act(X[:, k, xc0:xc0 + n], tm[:, 0:n], AF.Identity, [TK(3 + k % 2), "PAR"], xk,
                        bias=PAR[:, k, 24 + l:25 + l], scale=PAR[:, k, 20 + l:21 + l])
        return comp

    for l in range(depth):
        for g in range(6):
            stages.append(ada_stage(l, g))
        for hf in range(2):
            stages.append((None, h_stage(l, hf)))
            for j in range(8):
                stages.append(a_stage(l, hf, j) + ("A",))
            for i in range(8):
                stages.append(merge_stage(l, hf, i, 0))
            for j in range(8):
                stages.append(b_stage(l, hf, j) + ("B",))
            if hf == 1:
                stages.append((None, conv_out(l)))
            for i in range(8):
                stages.append(merge_stage(l, hf, i, 1))
            for i in range(8):
                stages.append(o_stage(l, hf, i))
            stages.append((None, ln_stage(l, hf)))

    wst = [s for s in stages if s[0] is not None]
    widx = {id(s): n for n, s in enumerate(wst)}
    PRE = 2
    for n in range(min(PRE, len(wst))):
        wst[n][0](n % 3)
    for s in stages:
        if s[0] is None:
            s[1]()
        else:
            n = widx[id(s)]
            if n + PRE < len(wst):
                wst[n + PRE][0]((n + PRE) % 3)
            s[1](n % 3)

    for b in range(nblk):
        sl = b % 4
        rows = 128 if b < SEQ // 128 else NSS * TS
        dst = yp[b * 128:(b + 1) * 128, :] if b < SEQ // 128 else ys
        for g in range(2):
            bank = 2 + g
            for k4 in range(4):
                k = g * 4 + k4
                tr(PS[bank][0:rows, k4 * 128:(k4 + 1) * 128], X[:, k, b * 128:b * 128 + rows], IDF[:], [("X", k, b), "IDF"], pk(bank))
            eng = "act" if g == 0 else "dve"
            cp(eng, XIN[sl][0:rows, g, :], PS[bank][0:rows, :], pk(bank), XKG[sl][g])
        dma("sp", dst.rearrange("p (g n) -> p g n", g=2), XIN[sl][0:rows, :, :], XK[sl], [("yout", sl)], f"sty{sl}")
    out_res.extend([("yout", i) for i in range(4)])
    rec_target[0] = None
    est = emit_scheduled(blk_all)
    if SCHED_DEBUG:
        print("sched ops", len(blk_all), "est_us", round(est, 1))
    A("sp", None, out_res, ())

    P.plan()
    sems = {k: es.enter_context(nc.semaphore(str(k))) for k in P.semkeys}
    block = es.enter_context(nc.Block())

    @block.sync
    def _(e): P.emit_engine("sp", e, sems)

    @block.scalar
    def _(e): P.emit_engine("act", e, sems)

    @block.vector
    def _(e): P.emit_engine("dve", e, sems)

    @block.gpsimd
    def _(e): P.emit_engine("pool", e, sems)

    @block.tensor
    def _(e): P.emit_engine("pe", e, sems)

    es.close()
    return nc


def make_in_maps(inp):
    f = lambda a: np.ascontiguousarray(np.asarray(a, dtype=np.float32))
    shared = {
        "w_ada": f(inp["w_ada"]), "b_ada": f(inp["b_ada"]).reshape(DEPTH * 3, 1024), "w_in": f(inp["w_in"]),
        "hgrn_lb": f(inp["hgrn_lb"]), "hgrn_gnorm": f(inp["hgrn_gnorm"]), "conv_w": f(inp["conv_w"]).reshape(DEPTH * 3, 1024),
        "w_a_out": f(inp["w_a_out"]), "w_b_out": f(inp["w_b_out"]), "w_o": f(inp["w_o"]),
        "ln_g": f(inp["ln_g"]), "ln_b": f(inp["ln_b"]),
    }
    xp = f(inp["x_prompt"]); xs = f(inp["x_sample"]); sh = f(inp["state_hgrn"]); sc = f(inp["state_conv"])
    cpr = f(inp["c_prompt"]); csm = f(inp["c_sample"])
    maps = []
    for c in range(NCORES):
        s0, s1 = c * NSS, (c + 1) * NSS
        m = dict(shared)
        m["xp"] = xp[c]
        m["xs"] = np.ascontiguousarray(xs[s0:s1].reshape(NSS * TS, 1024))
        m["sh"] = np.ascontiguousarray(sh[:, s0:s1])
        m["scv"] = np.ascontiguousarray(sc[:, s0:s1].reshape(DEPTH * NSS * 2, 1024))
        m["cc"] = np.ascontiguousarray(np.concatenate([cpr[c:c + 1], csm[s0:s1]], axis=0))
        maps.append(m)
    return maps


def gather(results):
    y_prompt = np.stack([r["yp"] for r in results], axis=0)
    y_sample = np.concatenate([r["ys"].reshape(NSS, TS, 1024) for r in results], axis=0)
    nhp = np.stack([r["nhp"] for r in results], axis=1)
    ncp = np.stack([r["ncp"].reshape(DEPTH, 2, 1024) for r in results], axis=1)
    nhs = np.concatenate([r["nhs"] for r in results], axis=1)
    ncs = np.concatenate([r["ncs"].reshape(DEPTH, NSS, 2, 1024) for r in results], axis=1)
    return tuple(np.ascontiguousarray(a, dtype=np.float32) for a in (y_prompt, y_sample, nhp, ncp, nhs, ncs))


_NC = None


def kernel(**inputs):
    global _NC
    if _NC is None:
        _NC = build_nc()
    res = run_bass_kernel_spmd(_NC, make_in_maps(inputs), core_ids=list(range(NCORES)))
    return gather(res.results)
```
